# Optimizing a Trainium2 kernel written in Bass

```python
import math
import jax, jax.numpy as jnp
from jax import lax
import numpy as np

D_MODEL = 1024
BATCH = 8
SEQ = 2048
DEPTH = 1
DEC_BATCH = 16
DEC_SEQ = 4096
PAST_LEN = 128

D_INNER = 2 * D_MODEL
HEAD_DIM = 64
SSD_HEADS = D_INNER // HEAD_DIM
SSD_GROUPS = 8
HEADS_PER_GROUP = SSD_HEADS // SSD_GROUPS
D_STATE = 128
CONV_WIDTH = 7
CONV_DIM = D_INNER + 2 * SSD_GROUPS * D_STATE
CHUNK = 128
NORM_GROUP = D_INNER // SSD_GROUPS
D_FOURIER = D_MODEL
FOURIER_GROUP_DIM = 128
FOURIER_GROUPS = D_FOURIER // FOURIER_GROUP_DIM
N_IN = D_INNER + CONV_DIM + 2 * SSD_HEADS + D_FOURIER + 2 * D_MODEL
D_FF = -(-8 * D_MODEL // (3 * 256)) * 256
EPS = 1e-5

kernel_name = 'hybrid_bidir_ssd_fnet_block'


def _rmsnorm(x, g):
    xf = x.astype(jnp.float32)
    y = xf * lax.rsqrt(jnp.mean(xf * xf, axis=-1, keepdims=True) + EPS)
    return (y * g.astype(jnp.float32)).astype(x.dtype)


def _centred_dwconv(u, w, b):
    pad = CONV_WIDTH // 2
    y = lax.conv_general_dilated(u, w[:, None, :].astype(u.dtype), window_strides=(1,),
                                 padding=[(pad, pad)], dimension_numbers=('NWC', 'WIO', 'NWC'),
                                 feature_group_count=u.shape[-1])
    return y + b


def _ssd_single(xh, dt, bm, cm, a):
    s = xh.shape[0]
    c = s // CHUNK
    xdt = (xh * dt[..., None]).reshape(c, CHUNK, SSD_GROUPS, HEADS_PER_GROUP, HEAD_DIM)
    cum = jnp.cumsum((dt * a).reshape(c, CHUNK, SSD_GROUPS, HEADS_PER_GROUP), axis=1)
    bc = bm.reshape(c, CHUNK, SSD_GROUPS, D_STATE)
    cc = cm.reshape(c, CHUNK, SSD_GROUPS, D_STATE)
    lower = jnp.tril(jnp.ones((CHUNK, CHUNK), dtype=bool))[None, :, :, None, None]
    seg = cum[:, :, None] - cum[:, None, :]
    decay = jnp.exp(jnp.where(lower, seg, -jnp.inf))
    scores = jnp.einsum('clgn,csgn->clsg', cc, bc)
    y_diag = jnp.einsum('clsg,clsgr,csgrp->clgrp', scores, decay, xdt)
    to_end = jnp.exp(cum[:, -1:] - cum)
    states = jnp.einsum('clgn,clgr,clgrp->cgrpn', bc, to_end, xdt)
    chunk_decay = jnp.exp(cum[:, -1])

    def step(h, inp):
        st, dec = inp
        return h * dec[..., None, None] + st, h

    h0 = jnp.zeros((SSD_GROUPS, HEADS_PER_GROUP, HEAD_DIM, D_STATE), jnp.float32)
    _, h_in = lax.scan(step, h0, (states, chunk_decay))
    y_off = jnp.einsum('clgn,cgrpn,clgr->clgrp', cc, h_in, jnp.exp(cum))
    return (y_diag + y_off).reshape(s, SSD_GROUPS, HEADS_PER_GROUP, HEAD_DIM)


def _ssd(xs, dt, bm, cm, a):
    return lax.map(lambda t: _ssd_single(t[0], t[1], t[2], t[3], a), (xs, dt, bm, cm))


def _block(x, norm_mix, w_in, conv_w, conv_b, dt_bias_f, dt_bias_b, a_log_f, a_log_b, d_skip,
           ssd_norm, w_ssd_out, w_fourier_out, b_fourier_out, w_out, norm_ffn, w_gate_up, w_down):
    bsz, s, _ = x.shape
    f32 = jnp.float32
    h = _rmsnorm(x, norm_mix)
    proj = h @ w_in
    o = 0
    z = proj[..., o:o + D_INNER]; o += D_INNER
    xbc = proj[..., o:o + CONV_DIM]; o += CONV_DIM
    dt_raw = proj[..., o:o + 2 * SSD_HEADS]; o += 2 * SSD_HEADS
    u = proj[..., o:o + D_FOURIER]; o += D_FOURIER
    gate_logits = proj[..., o:o + 2 * D_MODEL]

    xbc = jax.nn.silu(_centred_dwconv(xbc, conv_w, conv_b)).astype(f32)
    gn = SSD_GROUPS * D_STATE
    xs = xbc[..., :D_INNER].reshape(bsz, s, SSD_GROUPS, HEADS_PER_GROUP, HEAD_DIM)
    bm = xbc[..., D_INNER:D_INNER + gn].reshape(bsz, s, SSD_GROUPS, D_STATE)
    cm = xbc[..., D_INNER + gn:].reshape(bsz, s, SSD_GROUPS, D_STATE)
    dt = jax.nn.softplus(dt_raw.astype(f32) + jnp.concatenate([dt_bias_f, dt_bias_b]).astype(f32))
    dt_f = dt[..., :SSD_HEADS].reshape(bsz, s, SSD_GROUPS, HEADS_PER_GROUP)
    dt_b = dt[..., SSD_HEADS:].reshape(bsz, s, SSD_GROUPS, HEADS_PER_GROUP)
    a_f = -jnp.exp(a_log_f.astype(f32)).reshape(SSD_GROUPS, HEADS_PER_GROUP)
    a_b = -jnp.exp(a_log_b.astype(f32)).reshape(SSD_GROUPS, HEADS_PER_GROUP)
    y_f = _ssd(xs, dt_f, bm, cm, a_f)
    rev = lambda t: jnp.flip(t, axis=1)
    y_b = rev(_ssd(rev(xs), rev(dt_b), rev(bm), rev(cm), a_b))
    y = y_f + y_b + d_skip.astype(f32).reshape(SSD_GROUPS, HEADS_PER_GROUP)[..., None] * xs
    y = y.reshape(bsz, s, D_INNER) * jax.nn.silu(z.astype(f32))
    yg = y.reshape(bsz, s, SSD_GROUPS, NORM_GROUP)
    yg = yg * lax.rsqrt(jnp.mean(yg * yg, axis=-1, keepdims=True) + EPS)
    y = (yg.reshape(bsz, s, D_INNER) * ssd_norm.astype(f32)).astype(x.dtype)
    a_out = y @ w_ssd_out

    uf = u.astype(f32).reshape(bsz, s, FOURIER_GROUPS, FOURIER_GROUP_DIM)
    mixed = jnp.fft.fft2(uf, axes=(1, 3), norm='ortho').real
    mixed = mixed.reshape(bsz, s, D_FOURIER).astype(x.dtype)
    f_out = mixed @ w_fourier_out + b_fourier_out

    gates = jax.nn.sigmoid(gate_logits.astype(f32))
    merged = (gates[..., :D_MODEL] * a_out.astype(f32) + gates[..., D_MODEL:] * f_out.astype(f32)).astype(x.dtype)
    x = x + merged @ w_out

    h2 = _rmsnorm(x, norm_ffn)
    gu = h2 @ w_gate_up
    x = x + (jax.nn.silu(gu[..., :D_FF]) * gu[..., D_FF:]) @ w_down
    return x


def _trunk(x, layer_params, norm_final):
    for i in range(DEPTH):
        x = _block(x, *[p[i] for p in layer_params])
    return _rmsnorm(x, norm_final)


def _dt_bias_init(k, shape):
    u = jax.random.uniform(k, shape, jnp.float32)
    dt = jnp.exp(u * (math.log(0.1) - math.log(0.001)) + math.log(0.001))
    return dt + jnp.log(-jnp.expm1(-dt))


def setup_inputs(seed: int = 0) -> dict:
    key = jax.random.key(seed)
    ks = jax.random.split(key, 20)
    nrm = lambda k, shape, scale: jax.random.normal(k, shape, jnp.float32) * scale
    L = DEPTH
    return {
        'x_prompt': nrm(ks[0], (BATCH, SEQ, D_MODEL), 1.0),
        'x_sample': nrm(ks[1], (DEC_BATCH, DEC_SEQ, D_MODEL), 1.0),
        'norm_mix': 1.0 + nrm(ks[2], (L, D_MODEL), 0.01),
        'w_in': nrm(ks[3], (L, D_MODEL, N_IN), D_MODEL ** -0.5),
        'conv_w': nrm(ks[4], (L, CONV_WIDTH, CONV_DIM), CONV_WIDTH ** -0.5),
        'conv_b': nrm(ks[5], (L, CONV_DIM), 0.01),
        'dt_bias_f': _dt_bias_init(ks[6], (L, SSD_HEADS)),
        'dt_bias_b': _dt_bias_init(ks[7], (L, SSD_HEADS)),
        'a_log_f': jnp.log(jax.random.uniform(ks[8], (L, SSD_HEADS), jnp.float32, 1.0, 16.0)),
        'a_log_b': jnp.log(jax.random.uniform(ks[9], (L, SSD_HEADS), jnp.float32, 1.0, 16.0)),
        'd_skip': 1.0 + nrm(ks[10], (L, SSD_HEADS), 0.01),
        'ssd_norm': 1.0 + nrm(ks[11], (L, D_INNER), 0.01),
        'w_ssd_out': nrm(ks[12], (L, D_INNER, D_MODEL), D_INNER ** -0.5),
        'w_fourier_out': nrm(ks[13], (L, D_FOURIER, D_MODEL), D_FOURIER ** -0.5),
        'b_fourier_out': nrm(ks[14], (L, D_MODEL), 0.01),
        'w_out': nrm(ks[15], (L, D_MODEL, D_MODEL), D_MODEL ** -0.5),
        'norm_ffn': 1.0 + nrm(ks[16], (L, D_MODEL), 0.01),
        'w_gate_up': nrm(ks[17], (L, D_MODEL, 2 * D_FF), D_MODEL ** -0.5),
        'w_down': nrm(ks[18], (L, D_FF, D_MODEL), D_FF ** -0.5),
        'norm_final': 1.0 + nrm(ks[19], (D_MODEL,), 0.01),
    }


def reference(x_prompt, x_sample, norm_mix, w_in, conv_w, conv_b, dt_bias_f, dt_bias_b, a_log_f, a_log_b,
              d_skip, ssd_norm, w_ssd_out, w_fourier_out, b_fourier_out, w_out, norm_ffn, w_gate_up, w_down,
              norm_final):
    layer_params = (norm_mix, w_in, conv_w, conv_b, dt_bias_f, dt_bias_b, a_log_f, a_log_b, d_skip,
                    ssd_norm, w_ssd_out, w_fourier_out, b_fourier_out, w_out, norm_ffn, w_gate_up, w_down)
    y_prompt = _trunk(x_prompt, layer_params, norm_final)
    y_sample = _trunk(x_sample, layer_params, norm_final)
    return (y_prompt, y_sample)
```

```python
import math
import os
from contextlib import ExitStack
import numpy as np
import ml_dtypes
import concourse.bass as bass
import concourse.mybir as mybir
from concourse.bass_utils import run_bass_kernel_spmd

F32 = mybir.dt.float32
BF16 = mybir.dt.bfloat16
AF = mybir.ActivationFunctionType
ALU = mybir.AluOpType

D = 1024
DI = 2048
NIN = 9280
DFF = 2816
EPS = 1e-5
OFF_Z, OFF_X, OFF_B, OFF_C, OFF_DT, OFF_U, OFF_G = 0, 2048, 4096, 5120, 6144, 6208, 7232
NEG = -30000.0


class _Op:
    __slots__ = ("eng", "fn", "deps", "is_dma", "sem", "val", "lane_prev", "cost", "cost_given")


class Sched:
    ENGS = ("pe", "act", "dve", "pool", "sp")
    BLK = {"pe": "tensor", "act": "scalar", "dve": "vector", "pool": "gpsimd", "sp": "sync"}

    def __init__(self, nc, n_lanes=16, epoch=30000):
        self.nc = nc
        self.epoch = epoch
        self.n_lanes = n_lanes
        self.lane_sem = [nc.alloc_semaphore("lane%d" % i) for i in range(n_lanes)]
        self.lane_val = [0] * n_lanes
        self.next_lane = 0
        self.cur_sem = {e: nc.alloc_semaphore("eng_%s_0" % e) for e in self.ENGS}
        self.cur_val = {e: 0 for e in self.ENGS}
        self.nsem = {e: 1 for e in self.ENGS}
        self.waited = {e: {} for e in self.ENGS}
        self.ops = []
        self.last_writer = {}
        self.readers = {}
        self.n_emitted = 0

    def capture(self, f):
        self._cap = []
        f()
        lst = self._cap
        self._cap = None
        return lst

    def emit_merged(self, A, B):
        na, nb = len(A), len(B)
        ia = ib = 0
        while ia < na or ib < nb:
            if ib >= nb or (ia < na and ia * nb <= ib * na):
                self.op(*A[ia])
                ia += 1
            else:
                self.op(*B[ib])
                ib += 1

    DEF_COST = {"pe": 0.3, "act": 0.45, "dve": 0.4, "pool": 0.5, "sp": 0.05}

    def op(self, eng, fn, reads=(), writes=(), dma=False, cost=None):
        if getattr(self, "_cap", None) is not None:
            self._cap.append((eng, fn, tuple(reads), tuple(writes), dma, cost))
            return -1
        i = len(self.ops)
        deps = set()
        for k in reads:
            w = self.last_writer.get(k)
            if w is not None:
                deps.add(w)
        for k in writes:
            w = self.last_writer.get(k)
            if w is not None:
                deps.add(w)
            for r in self.readers.get(k, ()):
                deps.add(r)
        for k in reads:
            self.readers.setdefault(k, []).append(i)
        for k in writes:
            self.last_writer[k] = i
            self.readers[k] = []
        o = _Op()
        o.eng = eng
        o.fn = fn
        o.deps = deps
        o.is_dma = dma
        o.sem = None
        o.val = 0
        o.lane_prev = 0
        o.cost = cost if cost is not None else (2.5 if dma else self.DEF_COST[eng])
        o.cost_given = False
        self.ops.append(o)
        return i

    def dma(self, out, in_, reads=(), writes=(), eng="sp", cost=None):
        return self.op(eng, lambda e: e.dma_start(out=out, in_=in_), reads, writes, dma=True, cost=cost)

    class _Fake:
        class _Ins:
            def then_inc(self, *a, **k):
                return self

        TBL = None

        def __init__(self, eng):
            self.eng = eng
            self.t = 0.0
            self.tbl = None

        def __getattr__(self, name):
            def f(*args, **kw):
                if name == "activation":
                    fn_ = kw.get("func")
                    if fn_ in (AF.Silu, AF.Sigmoid):
                        self.tbl = "silu"
                    elif fn_ in (AF.Exp, AF.Ln):
                        self.tbl = "exp"
                    elif fn_ == AF.Sqrt:
                        self.tbl = "sqrt"
                try:
                    if name in ("matmul", "transpose"):
                        r = kw.get("rhs", kw.get("in_"))
                        n = r.free_size() if name == "matmul" else 128
                        mul = 4.0 if r.dtype == F32 else 1.0
                        self.t += 0.065 + mul * n / 2200.0
                    elif name == "dma_start":
                        self.t += 2.0 + kw["out"].nbytes() / 150e3
                    else:
                        o = kw.get("out", kw.get("ap", args[0] if args else None))
                        fs = o.free_size() if o is not None else 64
                        if self.eng == "act":
                            self.t += 0.22 + fs / 1300.0
                        elif self.eng == "dve":
                            self.t += 0.12 + fs / 900.0
                        else:
                            self.t += 0.2 + fs / 520.0
                except Exception:
                    self.t += 0.4
                return Sched._Fake._Ins()
            return f

    def _estimate(self, o):
        fk = Sched._Fake(o.eng)
        try:
            o.fn(fk)
        except Exception:
            return None
        self._tbl[id(o)] = fk.tbl
        return fk.t if fk.t > 0 else None

    def _reorder(self):
        self._tbl = {}
        for o in self.ops:
            if not getattr(o, "cost_given", False):
                est = self._estimate(o)
                if est is not None:
                    o.cost = est
        import heapq
        ops = self.ops
        n = len(ops)
        succ = [[] for _ in range(n)]
        indeg = [0] * n
        for i, o in enumerate(ops):
            indeg[i] = len(o.deps)
            for d in o.deps:
                succ[d].append(i)
        bl = [0.0] * n
        for i in range(n - 1, -1, -1):
            m_ = 0.0
            for s in succ[i]:
                if bl[s] > m_:
                    m_ = bl[s]
            bl[i] = ops[i].cost + m_ + (0.15 if succ[i] else 0.0)
        finish = [0.0] * n
        tbl_of = [self._tbl.get(id(o)) for o in ops]
        cur_tbl = [None]
        ready_l = {e: [] for e in self.ENGS}
        free = {e: 0.0 for e in self.ENGS}
        ready_t = [0.0] * n
        for i, o in enumerate(ops):
            if indeg[i] == 0:
                ready_l[o.eng].append(i)
        order = []
        while len(order) < n:
            best = None
            for e in self.ENGS:
                rl = ready_l[e]
                if not rl:
                    continue
                tmin = min(ready_t[i] for i in rl)
                t_e = max(free[e], tmin)
                cand = None
                cand_same = None
                for i in rl:
                    if ready_t[i] <= t_e + 1e-9:
                        if cand is None or (bl[i], -i) > (bl[cand], -cand):
                            cand = i
                        if e == "act":
                            tb_ = tbl_of[i]
                            if tb_ is None or tb_ == cur_tbl[0]:
                                if cand_same is None or (bl[i], -i) > (bl[cand_same], -cand_same):
                                    cand_same = i
                if e == "act" and cand_same is not None:
                    cand = cand_same
                if best is None or (t_e, cand) < (best[0], best[2]):
                    best = (t_e, e, cand)
            assert best is not None, "reorder: cyclic deps"
            st, e, i = best
            ready_l[e].remove(i)
            o = ops[i]
            if e == "act" and tbl_of[i] is not None and tbl_of[i] != cur_tbl[0]:
                st += 1.3
                cur_tbl[0] = tbl_of[i]
            if o.is_dma:
                free[e] = st + 0.08
                finish[i] = st + o.cost
            else:
                free[e] = st + o.cost
                finish[i] = st + o.cost
            order.append(i)
            for s in succ[i]:
                hop = 0.0 if ops[s].eng == e and not o.is_dma else 0.15
                ready_t[s] = max(ready_t[s], finish[i] + hop)
                indeg[s] -= 1
                if indeg[s] == 0:
                    ready_l[ops[s].eng].append(s)
        remap = {old: new for new, old in enumerate(order)}
        new_ops = [ops[i] for i in order]
        for o in new_ops:
            o.deps = {remap[d] for d in o.deps}
        for k, o in enumerate(new_ops):
            for d in o.deps:
                assert d < k, "reorder produced non-topological order"
        self.ops = new_ops
        self.sim_span = max(finish) if n else 0.0

    def _wait(self, engobj, e, sem, val):
        w = self.waited[e]
        if w.get(sem.num, 0) >= val:
            return
        w[sem.num] = val
        engobj.wait_ge(sem, val)

    def _check(self, ops, per):
        semv = {}
        ptr = {e: 0 for e in self.ENGS}
        base = {}
        for o in ops:
            if o.sem is not None and o.sem.num not in base:
                base[o.sem.num] = o.val - (16 if o.is_dma else 1)
        semv.update(base)
        progress = True
        done = 0
        while progress:
            progress = False
            for e in self.ENGS:
                while ptr[e] < len(per[e]):
                    o = ops[per[e][ptr[e]]]
                    ok = True
                    for d in o.deps:
                        p = ops[d]
                        if p.sem is None:
                            continue
                        if p.eng == "pe" and e == "pe" and not p.is_dma and not o.is_dma:
                            continue
                        if semv.get(p.sem.num, 0) < p.val:
                            ok = False
                            break
                    if ok and o.is_dma and o.lane_prev > 0 and semv.get(o.sem.num, 0) < o.lane_prev:
                        ok = False
                    if not ok:
                        break
                    if o.sem is not None:
                        semv[o.sem.num] = semv.get(o.sem.num, 0) + (16 if o.is_dma else 1)
                        assert semv[o.sem.num] <= o.val + 16 * self.n_lanes, "sem overshoot"
                    ptr[e] += 1
                    done += 1
                    progress = True
        if done != len(ops):
            stuck = {e: per[e][ptr[e]] for e in self.ENGS if ptr[e] < len(per[e])}
            raise RuntimeError("SCHED deadlock: %d/%d ops done, stuck at %s" % (done, len(ops), stuck))
        print("SCHED_CHECK ok", len(ops), flush=True)

    def flush(self, reorder=True):
        if reorder and os.environ.get("NO_REORDER") != "1":
            self._reorder()
            if os.environ.get("SCHED_CHECK") == "1":
                print("sim_span_us", round(self.sim_span, 1), flush=True)
        ops = self.ops
        n = len(ops)
        nc = self.nc
        consumed = [False] * n
        for o in ops:
            for d in o.deps:
                p = ops[d]
                if p.eng == "pe" and o.eng == "pe" and not p.is_dma and not o.is_dma:
                    continue
                consumed[d] = True
        last_of = {}
        for i, o in enumerate(ops):
            if not o.is_dma:
                last_of[o.eng] = i
        for i, o in enumerate(ops):
            if o.is_dma:
                lane = self.next_lane
                self.next_lane = (lane + 1) % self.n_lanes
                o.lane_prev = self.lane_val[lane]
                self.lane_val[lane] += 16
                o.sem = self.lane_sem[lane]
                o.val = self.lane_val[lane]
            elif consumed[i] or last_of[o.eng] == i:
                e = o.eng
                if self.cur_val[e] >= self.epoch:
                    self.cur_sem[e] = nc.alloc_semaphore("eng_%s_%d" % (e, self.nsem[e]))
                    self.nsem[e] += 1
                    self.cur_val[e] = 0
                self.cur_val[e] += 1
                o.sem = self.cur_sem[e]
                o.val = self.cur_val[e]
        per = {e: [] for e in self.ENGS}
        for i, o in enumerate(ops):
            per[o.eng].append(i)
        if os.environ.get("SCHED_CHECK") == "1":
            self._check(ops, per)
        final_eng = {e: (self.cur_sem[e], self.cur_val[e]) for e in self.ENGS}
        final_lane = list(self.lane_val)
        with nc.Block() as block:
            for e in self.ENGS:
                def body(engobj, e=e):
                    for i in per[e]:
                        o = ops[i]
                        for d in sorted(o.deps):
                            p = ops[d]
                            if p.sem is None:
                                continue
                            if p.eng == "pe" and e == "pe" and not p.is_dma and not o.is_dma:
                                continue
                            self._wait(engobj, e, p.sem, p.val)
                        if o.is_dma and o.lane_prev > 0:
                            self._wait(engobj, e, o.sem, o.lane_prev)
                        ins = o.fn(engobj)
                        if o.sem is not None:
                            ins.then_inc(o.sem, 16 if o.is_dma else 1)
                    for e2 in self.ENGS:
                        if e2 != e and final_eng[e2][1] > 0:
                            self._wait(engobj, e, final_eng[e2][0], final_eng[e2][1])
                    for l in range(self.n_lanes):
                        if final_lane[l] > 0:
                            self._wait(engobj, e, self.lane_sem[l], final_lane[l])
                getattr(block, self.BLK[e])(body)
        self.n_emitted += n
        self.ops = []
        self.last_writer = {}
        self.readers = {}


class _Alloc:
    def __init__(self, nc):
        self.nc = nc
        self.es = ExitStack()

    def __enter__(self):
        self.es.__enter__()
        return self

    def __exit__(self, *a):
        return self.es.__exit__(*a)

    _uid = [0]

    def _nm(self, name):
        self._uid[0] += 1
        return "%s_s%d" % (name, self._uid[0])

    def sb(self, name, shape, dt):
        return self.es.enter_context(self.nc.sbuf_tensor(self._nm(name), shape, dt))

    def ps(self, name, shape, dt):
        return self.es.enter_context(self.nc.psum_tensor(self._nm(name), shape, dt))


def bc(ap, shape):
    return ap.broadcast_to(list(shape))


class _Stop(Exception):
    pass


def build(seq_lens, dump=(), upto=None):
    try:
        return _build(seq_lens, dump, upto)
    except _Stop as s:
        return s.args[0]


def _build(seq_lens, dump=(), upto=None):
    nc = bass.Bass("TRN2", target_bir_lowering=False)
    nc.allow_low_precision("bf16 matmul operands, fp32 accumulation")
    Stot = sum(seq_lens)
    Smax = max(seq_lens)
    uniqS = sorted(set(seq_lens))

    def din(name, shape, dt=F32):
        return nc.dram_tensor(name, list(shape), dt, kind="ExternalInput").ap()

    x_all = din("x_all", [Stot, D])
    w_in_l = din("w_in_l", [128, 8, NIN])
    w_so_l = din("w_so_l", [128, 16, D])
    w_fo_l = din("w_fo_l", [128, 8, D])
    w_o_l = din("w_o_l", [128, 8, D])
    w_gu_l = din("w_gu_l", [128, 8, 2 * DFF])
    w_d_l = din("w_d_l", [128, 22, D])
    convw_l = din("convw_l", [128, 32, 7])
    convb_l = din("convb_l", [128, 32])
    dtb_col = din("dtb_col", [128, 1])
    alog_col = din("alog_col", [128, 1])
    dskip_bc = din("dskip_bc", [128, 32])
    ssdn_bc = din("ssdn_bc", [128, DI])
    gmix_col = din("gmix_col", [128, 8])
    gffn_col = din("gffn_col", [128, 8])
    gfin_bc = din("gfin_bc", [128, D])
    bfo_col = din("bfo_col", [128, 8])
    ident_b_d = din("ident_b", [128, 128], BF16)
    ident_f_d = din("ident_f", [128, 128])
    antiid_d = din("antiid_b", [128, 128], BF16)
    tri_f_d = din("tri_f", [128, 5, 128])
    tri_b_d = din("tri_b", [128, 4, 128], BF16)
    neg_b_d = din("neg_b", [128, 2, 512], BF16)
    cs128_d = {S: din("cs128_%d" % S, [128, 384], BF16) for S in uniqS}
    tab_d = {S: din("tab_%d" % S, [S // 128, 128, S // 128, 2, 128], BF16) for S in uniqS}

    y_all = nc.dram_tensor("y_all", [Stot, D], F32, kind="ExternalOutput").ap()

    def dscr(name, shape, dt):
        kind = "ExternalOutput" if name in dump else "Internal"
        return nc.dram_tensor(name, list(shape), dt, kind=kind).ap()

    gatesT = dscr("gatesT", [2048, Smax], BF16)
    Utm = dscr("Utm", [Smax, D], BF16)
    ynT = dscr("ynT", [DI, Smax], BF16)
    YT = dscr("YT", [D, Smax], BF16)
    x2d = dscr("x2d", [Smax, D], F32)

    S_ = Sched(nc)
    op = S_.op
    dma = S_.dma

    with _Alloc(nc) as A1:
        ident_b = A1.sb("ident_b", [128, 128], BF16)
        ident_f = A1.sb("ident_f", [128, 128], F32)
        tri_f = A1.sb("tri_f", [128, 5, 128], F32)
        tri_b = A1.sb("tri_b", [128, 4, 128], BF16)
        neg_b = A1.sb("neg_b", [128, 2, 512], BF16)
        gmix = A1.sb("gmix", [128, 8], F32)
        gffn = A1.sb("gffn", [128, 8], F32)
        bfo = A1.sb("bfo", [128, 8], F32)
        dtb = A1.sb("dtb", [128, 1], F32)
        acol = A1.sb("acol", [128, 1], F32)
        dsk = A1.sb("dsk", [128, 32], F32)
        dma(ident_b[:], ident_b_d, writes=["c_idb"])
        dma(ident_f[:], ident_f_d, writes=["c_idf"])
        dma(tri_f[:], tri_f_d, writes=["c_trif"])
        dma(tri_b[:], tri_b_d, writes=["c_trib"])
        dma(neg_b[:], neg_b_d, writes=["c_negb"])
        dma(gmix[:], gmix_col, writes=["c_gmix"])
        dma(gffn[:], gffn_col, writes=["c_gffn"])
        dma(bfo[:], bfo_col, writes=["c_bfo"])
        dma(dtb[:], dtb_col, writes=["c_dtb"])
        dma(acol[:], alog_col, writes=["c_acol"])
        dma(dsk[:], dskip_bc, writes=["c_dsk"])
        op("act", lambda e: e.activation(out=acol[:], in_=acol[:], func=AF.Exp), reads=["c_acol"], writes=["c_acol"])
        op("dve", lambda e: e.tensor_scalar(out=acol[:], in0=acol[:], scalar1=-1.0, scalar2=None, op0=ALU.mult),
           reads=["c_acol"], writes=["c_acol"])
        S_.flush()
        if upto == 'const':
            raise _Stop(nc)

        def rms_to_T(tag, b, xt_ap, xn, ss, sd, rstd, pT_ap, gcol, dst_ap, rkeys, dst_key):
            op("act", lambda e: e.activation(out=xn[:], in_=xt_ap, func=AF.Square, accum_out=ss[:]),
               reads=rkeys, writes=[(tag, "xn", b), (tag, "ss", b)])
            op("act", lambda e: e.activation(out=sd[:], in_=ss[:], func=AF.Sqrt, scale=1.0 / D, bias=EPS),
               reads=[(tag, "ss", b)], writes=[(tag, "sd", b)])
            op("dve", lambda e: e.reciprocal(out=rstd[:], in_=sd[:]), reads=[(tag, "sd", b)], writes=[(tag, "rstd", b)])
            op("act", lambda e: e.activation(out=xn[:], in_=xt_ap, func=AF.Copy, scale=rstd[:]),
               reads=rkeys + [(tag, "rstd", b)], writes=[(tag, "xn", b)])

            def tr(e):
                for c in range(8):
                    ins = e.transpose(out=pT_ap[:, c, :], in_=xn[:, c * 128:(c + 1) * 128], identity=ident_b[:])
                return ins
            op("pe", tr, reads=[(tag, "xn", b)], writes=[(tag, "pT", b)])
            op("dve", lambda e: e.tensor_tensor(out=dst_ap, in0=pT_ap, in1=bc(gcol[:].unsqueeze(2), [128, 8, 128]),
                                                 op=ALU.mult),
               reads=[(tag, "pT", b)], writes=[dst_key])

        s0 = 0
        for si, S in enumerate(seq_lens):
            NT = S // 128
            NB = S // 512
            x_seq = x_all[s0:s0 + S, :]
            y_seq = y_all[s0:s0 + S, :]
            with _Alloc(nc) as A2:
                hT = A2.sb("hT", [128, 8, S], BF16)
                with _Alloc(nc) as A3:
                    xt = [A3.sb("xt%d" % k_, [128, D], F32) for k_ in range(4)]
                    xn = [A3.sb("xn%d" % k_, [128, D], BF16) for k_ in range(4)]
                    ss = [A3.sb("ss%d" % k_, [128, 1], F32) for k_ in range(4)]
                    sd = [A3.sb("sd%d" % k_, [128, 1], F32) for k_ in range(4)]
                    rs = [A3.sb("rs%d" % k_, [128, 1], F32) for k_ in range(4)]
                    pT1 = A3.ps("pT1", [128, 4, 8, 128], BF16)
                    for t in range(NT):
                        b = t % 4
                        dma(xt[b][:], x_seq[t * 128:(t + 1) * 128, :], writes=[("p1", "xt", b)])
                        rms_to_T("p1", b, xt[b][:], xn[b], ss[b], sd[b], rs[b], pT1[:, b], gmix,
                                 hT[:, :, t * 128:(t + 1) * 128], [("p1", "xt", b)], ("hT", t))
                    S_.flush()
                    if upto == 'p1':
                        raise _Stop(nc)

                with _Alloc(nc) as A4:
                    dt_tm = A4.sb("dt_tm", [128, NT, 128], F32)
                    with _Alloc(nc) as A5:
                        wt0 = A5.sb("wt0", [128, 8, 128], BF16)
                        wt1 = A5.sb("wt1", [128, 8, 128], BF16)
                        grow0 = A5.sb("grow0", [128, S], BF16)
                        grow1 = A5.sb("grow1", [128, S], BF16)
                        wu = A5.sb("wu", [128, 8, D], BF16)
                        urow0 = A5.sb("urow0", [128, D], BF16)
                        urow1 = A5.sb("urow1", [128, D], BF16)
                        wdt = A5.sb("wdt", [128, 8, 128], BF16)
                        dtx = A5.sb("dtx", [128, 512], F32)
                        dta = A5.sb("dta", [128, 512], F32)
                        dte = A5.sb("dte", [128, 512], F32)
                        dts = A5.sb("dts", [128, 512], F32)
                        psA = A5.ps("psA", [128, 4, 512], F32)
                        psU = A5.ps("psU", [128, 2, 2, 512], F32)
                        wt = [wt0, wt1]; grow = [grow0, grow1]; urow = [urow0, urow1]
                        nmm = 0
                        for j in range(16):
                            b = j % 2
                            c0 = OFF_G + j * 128
                            dma(wt[b][:], w_in_l[:, :, c0:c0 + 128], writes=[("wt", b)], eng="pool")
                            for blk in range(NB):
                                bank = nmm % 4
                                nmm += 1

                                def mm(e, b=b, blk=blk, bank=bank):
                                    for c in range(8):
                                        ins = e.matmul(psA[:, bank, :], lhsT=wt[b][:, c, :],
                                                       rhs=hT[:, c, blk * 512:(blk + 1) * 512], start=(c == 0), stop=(c == 7))
                                    return ins
                                op("pe", mm, reads=[("wt", b)] + [("hT", t) for t in range(blk * 4, blk * 4 + 4)],
                                   writes=[("psA", bank)])
                                op("act", lambda e, b=b, blk=blk, bank=bank: e.activation(
                                    out=grow[b][:, blk * 512:(blk + 1) * 512], in_=psA[:, bank, :], func=AF.Sigmoid),
                                   reads=[("psA", bank)], writes=[("grow", b, blk)])
                            dma(gatesT[j * 128:(j + 1) * 128, 0:S], grow[b][:],
                                reads=[("grow", b, blk) for blk in range(NB)], writes=[("gatesT", j)])
                        dma(wu[:], w_in_l[:, :, OFF_U:OFF_U + D], writes=["wu"], eng="pool")
                        for t in range(NT):
                            b = t % 2

                            def mmu(e, t=t, b=b):
                                for hf in range(2):
                                    for c in range(8):
                                        ins = e.matmul(psU[:, b, hf, :], lhsT=hT[:, c, t * 128:(t + 1) * 128],
                                                       rhs=wu[:, c, hf * 512:(hf + 1) * 512], start=(c == 0), stop=(c == 7))
                                return ins
                            op("pe", mmu, reads=["wu", ("hT", t)], writes=[("psU", b)])
                            op("dve", lambda e, b=b: e.tensor_copy(out=urow[b][:], in_=psU[:, b].rearrange("p a b -> p (a b)")),
                               reads=[("psU", b)], writes=[("urow", b)])
                            dma(Utm[t * 128:(t + 1) * 128, :], urow[b][:], reads=[("urow", b)], writes=[("Utm", t)])
                        dma(wdt[:, :, 0:64], w_in_l[:, :, OFF_DT:OFF_DT + 64], writes=["wdt"], eng="pool")
                        dma(wdt[:, :, 64:128], w_in_l[:, :, OFF_DT:OFF_DT + 64], writes=["wdt2"], eng="pool")
                        for blk in range(NB):
                            bank = nmm % 4
                            nmm += 1

                            def mmd(e, blk=blk, bank=bank):
                                for c in range(8):
                                    ins = e.matmul(psA[:, bank, :], lhsT=wdt[:, c, :],
                                                   rhs=hT[:, c, blk * 512:(blk + 1) * 512], start=(c == 0), stop=(c == 7))
                                return ins
                            op("pe", mmd, reads=["wdt", "wdt2"] + [("hT", t) for t in range(blk * 4, blk * 4 + 4)],
                               writes=[("psA", bank)])
                            op("act", lambda e, bank=bank: e.activation(out=dtx[:], in_=psA[:, bank, :], func=AF.Identity,
                                                                         bias=dtb[:]),
                               reads=[("psA", bank)], writes=["dtx"])
                            op("act", lambda e: e.activation(out=dta[:], in_=dtx[:], func=AF.Abs),
                               reads=["dtx"], writes=["dta"])
                            op("act", lambda e: e.activation(out=dte[:], in_=dta[:], func=AF.Exp, scale=-1.0),
                               reads=["dta"], writes=["dte"])
                            op("act", lambda e: e.activation(out=dte[:], in_=dte[:], func=AF.Ln, bias=1.0),
                               reads=["dte"], writes=["dte"])
                            op("dve", lambda e: e.scalar_tensor_tensor(out=dts[:], in0=dtx[:], scalar=0.0, in1=dte[:],
                                                                       op0=ALU.max, op1=ALU.add),
                               reads=["dtx", "dte"], writes=["dts"])
                            op("dve", lambda e: e.tensor_scalar(out=dts[64:128, :], in0=dts[64:128, :],
                                                                scalar1=acol[64:128, :], scalar2=None, op0=ALU.mult),
                               reads=["dts"], writes=["dts"])

                            def trd(e, blk=blk):
                                for q in range(4):
                                    ins = e.transpose(out=psU[:, 0, 0, q * 128:(q + 1) * 128],
                                                      in_=dts[:, q * 128:(q + 1) * 128], identity=ident_f[:])
                                return ins
                            op("pe", trd, reads=["dts"], writes=[("psU", 0)])
                            op("dve", lambda e, blk=blk: e.tensor_copy(
                                out=dt_tm[:, blk * 4:(blk + 1) * 4, :].rearrange("p a b -> p (a b)"), in_=psU[:, 0, 0, :]),
                               reads=[("psU", 0)], writes=[("dt_tm", blk)])
                        S_.flush()
                        if upto == 'p2a':
                            raise _Stop(nc)

                    with _Alloc(nc) as A6:
                        wta = A6.sb("wta", [128, 8, 128], BF16)
                        wtb = A6.sb("wtb", [128, 8, 128], BF16)
                        wz = A6.sb("wz", [128, 8, 256], BF16)
                        ppad = A6.sb("ppad", [128, S + 8], BF16)
                        szall = A6.sb("szall", [128, NT, 256], BF16)
                        xc0 = A6.sb("xc0", [128, S], BF16)
                        xc1 = A6.sb("xc1", [128, S], BF16)
                        xcB = A6.sb("xcB", [128, S], BF16)
                        xcC = A6.sb("xcC", [128, S], BF16)
                        hbsn = A6.sb("hbsn", [128, NT, 256], BF16)
                        ynTg0 = A6.sb("ynTg0", [128, 2, 512], BF16)
                        ynTg1 = A6.sb("ynTg1", [128, 2, 512], BF16)
                        ynTgs = [ynTg0, ynTg1]
                        ssdn = A6.sb("ssdn", [128, 256], F32)
                        cw = A6.sb("cw", [128, 32, 7], F32)
                        cb = A6.sb("cb", [128, 32], F32)
                        dg = A6.sb("dg", [128, 7, 128], BF16)
                        xb0 = A6.sb("xb0", [128, 384], BF16)
                        xb1 = A6.sb("xb1", [128, 384], BF16)
                        cps = A6.sb("cps", [128, NT, 16], F32)
                        evarg = A6.sb("evarg", [128, NT, 24], F32)
                        evall = A6.sb("evall", [128, NT, 24], F32)
                        cfall = A6.sb("cfall", [128, NT, 8], F32)
                        xws = [A6.sb("xw%d" % k_, [128, 256], BF16) for k_ in range(2)]
                        xdfs = [A6.sb("xdf%d" % k_, [128, 256], BF16) for k_ in range(2)]
                        xdbs = [A6.sb("xdb%d" % k_, [128, 256], BF16) for k_ in range(2)]
                        xDs = [A6.sb("xD%d" % k_, [128, 256], BF16) for k_ in range(2)]
                        rDf = A6.sb("rDf", [128, 4, 128], BF16)
                        rDb = A6.sb("rDb", [128, 4, 128], BF16)
                        Df = A6.sb("Df", [128, 4, 128], BF16)
                        Db = A6.sb("Db", [128, 4, 128], BF16)
                        Mf = A6.sb("Mf", [128, 4, 128], BF16)
                        Mb = A6.sb("Mb", [128, 4, 128], BF16)
                        t1s = [A6.sb("t1%d" % k_, [128, 256], F32) for k_ in range(2)]
                        t2s = [A6.sb("t2%d" % k_, [128, 256], F32) for k_ in range(2)]
                        yzs = [A6.sb("yz%d" % k_, [128, 256], F32) for k_ in range(2)]
                        sqj = A6.sb("sqj", [128, 256], BF16)
                        ssq = A6.sb("ssq", [128, 1], F32)
                        lnv = A6.sb("lnv", [128, 1], F32)
                        rsv = A6.sb("rsv", [128, 1], F32)
                        yn = A6.sb("yn", [128, 256], BF16)
                        Hf = A6.sb("Hf", [128, 256], F32)
                        Hb = A6.sb("Hb", [128, 256], F32)
                        Hfbs = [A6.sb("Hfb%d" % k_, [128, 256], BF16) for k_ in range(2)]
                        qA = A6.ps("qA", [128, 2, 512], F32)
                        qO = A6.ps("qO", [128, 512], F32)
                        qI = A6.ps("qI", [128, 512], F32)
                        qC = A6.ps("qC", [128, 2, 512], F32)
                        qD = A6.ps("qD", [128, 2, 512], F32)
                        wtt = [wta, wtb]
                        xb = [xb0, xb1]
                        if os.environ.get("SCHED_CHECK") == "1":
                            print("P2b sbuf remaining", nc.sbuf_bytes_remaining, flush=True)
                        dma(cw[:], convw_l, writes=["cw"])
                        dma(cb[:], convb_l, writes=["cb"])
                        op("pool", lambda e: e.memset(ppad[:], 0.0), writes=["ppad_pad"])
                        cnt = {'mm': 0, 'w': 0}
                        def do_group(g):
                            nonlocal_cnt = cnt
                            tiles = [(OFF_X + (2 * g) * 128, xc0, "x0"), (OFF_X + (2 * g + 1) * 128, xc1, "x1"),
                                     (OFF_B + g * 128, xcB, "B"), (OFF_C + g * 128, xcC, "C")]
                            dma(wz[:], w_in_l[:, :, OFF_Z + g * 256:OFF_Z + (g + 1) * 256], writes=["wz"], eng="pool")
                            dma(ssdn[:], ssdn_bc[:, g * 256:(g + 1) * 256], writes=["ssdn"])
                            for (c0, xc, nm) in tiles:
                                ct = (c0 - OFF_X) // 128
                                wb = cnt['w'] % 2
                                cnt['w'] += 1
                                dma(wtt[wb][:], w_in_l[:, :, c0:c0 + 128], writes=[("wtt", wb)], eng="pool")
                                for blk in range(NB):
                                    bank = cnt['mm'] % 2
                                    cnt['mm'] += 1

                                    def mm(e, wb=wb, blk=blk, bank=bank):
                                        for c in range(8):
                                            ins = e.matmul(qI[:], lhsT=wtt[wb][:, c, :],
                                                           rhs=hT[:, c, blk * 512:(blk + 1) * 512], start=(c == 0), stop=(c == 7))
                                        return ins
                                    op("pe", mm, reads=[("wtt", wb)], writes=["bk_qI"])
                                    op("act", lambda e, blk=blk, bank=bank: e.activation(
                                        out=ppad[:, 3 + blk * 512:3 + (blk + 1) * 512], in_=qI[:], func=AF.Copy),
                                       reads=["ppad_pad"], writes=[("ppad", blk), "bk_qI"])
                                op("dve", lambda e, ct=ct: e.tensor_tensor(
                                    out=dg[:], in0=bc(ident_b[:].unsqueeze(1), [128, 7, 128]),
                                    in1=bc(cw[:, ct, :].unsqueeze(2), [128, 7, 128]), op=ALU.mult),
                                   reads=["cw"], writes=["dg"])
                                for blk in range(NB):
                                    bank = cnt['mm'] % 2
                                    cnt['mm'] += 1

                                    def mmc(e, blk=blk, bank=bank):
                                        for k in range(7):
                                            ins = e.matmul(qI[:], lhsT=dg[:, k, :],
                                                           rhs=ppad[:, blk * 512 + k:blk * 512 + k + 512],
                                                           start=(k == 0), stop=(k == 6))
                                        return ins
                                    rk = [("ppad", b2) for b2 in range(max(0, blk - 1), min(NB, blk + 2))]
                                    op("pe", mmc, reads=["dg", "ppad_pad"] + rk, writes=["bk_qI"])
                                    op("act", lambda e, blk=blk, bank=bank, xc=xc, ct=ct: e.activation(
                                        out=xc[:, blk * 512:(blk + 1) * 512], in_=qI[:], func=AF.Silu,
                                        bias=cb[:, ct:ct + 1]),
                                       reads=["cb"], writes=[("xc", nm, blk), "bk_qI"])
                            if float(os.environ.get('DBG_LVL', '9')) < 1:
                                return
                            gf = slice(4 * g, 4 * g + 4)
                            gb = slice(32 + 4 * g, 32 + 4 * g + 4)
                            af = slice(64 + 4 * g, 64 + 4 * g + 4)
                            ab = slice(96 + 4 * g, 96 + 4 * g + 4)

                            def x4(ap):
                                return ap.rearrange("p (h d) -> p h d", h=4)

                            def b4(ap):
                                return bc(ap.unsqueeze(2), [128, 4, 64])

                            def csall(e):
                                for c in range(NT):
                                    rhs_ = dt_tm[:, c, 64:128].rearrange("p (d h) -> p d h", d=2)[:, :, 4 * g:4 * g + 4]
                                    e.matmul(qO[:, c * 16:c * 16 + 8].rearrange("p (d h) -> p d h", d=2), lhsT=tri_f[:, 0, :],
                                             rhs=rhs_, start=True, stop=True)
                                    ins = e.matmul(qO[:, c * 16 + 8:c * 16 + 16].rearrange("p (d h) -> p d h", d=2),
                                                   lhsT=tri_f[:, 4, :], rhs=rhs_, start=True, stop=True)
                                return ins
                            op("pe", csall, reads=[("dt_tm", b_) for b_ in range(NB)], writes=["qO"], cost=0.8 * NT)
                            op("act", lambda e: e.activation(out=cps[:].rearrange("p c k -> p (c k)"), in_=qO[:, 0:NT * 16],
                                                             func=AF.Copy),
                               reads=["qO"], writes=["cps"])
                            op("dve", lambda e: e.tensor_copy(out=evarg[:, :, 0:4], in_=cps[:, :, 0:4]), reads=["cps"], writes=["ea0"])
                            op("dve", lambda e: e.tensor_tensor(out=evarg[:, :, 4:8], in0=cps[:, :, 8:12], in1=cps[:, :, 0:4],
                                                                op=ALU.subtract), reads=["cps"], writes=["ea1"])
                            op("dve", lambda e: e.tensor_copy(out=evarg[:, :, 8:12], in_=cps[:, :, 8:12]), reads=["cps"], writes=["ea2"])
                            op("dve", lambda e: e.tensor_tensor(out=evarg[:, :, 16:20], in0=cps[:, :, 4:8], in1=dt_tm[:, :, ab],
                                                                op=ALU.subtract),
                               reads=["cps"] + [("dt_tm", b_) for b_ in range(NB)], writes=["ea4"])
                            op("dve", lambda e: e.tensor_tensor(out=evarg[:, :, 12:16], in0=cps[:, :, 12:16], in1=evarg[:, :, 16:20],
                                                                op=ALU.subtract), reads=["cps", "ea4"], writes=["ea3"])
                            op("dve", lambda e: e.tensor_copy(out=evarg[:, :, 20:24], in_=cps[:, :, 12:16]), reads=["cps"], writes=["ea5"])
                            op("act", lambda e: e.activation(out=evall[:].rearrange("p c k -> p (c k)"),
                                                             in_=evarg[:].rearrange("p c k -> p (c k)"), func=AF.Exp),
                               reads=["ea0", "ea1", "ea2", "ea3", "ea4", "ea5"], writes=["evall"])
                            op("dve", lambda e: e.tensor_tensor(out=cfall[:, :, 0:4], in0=dt_tm[:, :, gf], in1=evall[:, :, 4:8],
                                                                op=ALU.mult),
                               reads=["evall"] + [("dt_tm", b_) for b_ in range(NB)], writes=["cfall0"])
                            op("dve", lambda e: e.tensor_tensor(out=cfall[:, :, 4:8], in0=dt_tm[:, :, gb], in1=evall[:, :, 16:20],
                                                                op=ALU.mult),
                               reads=["evall"] + [("dt_tm", b_) for b_ in range(NB)], writes=["cfall1"])

                            for c in range(NT):
                                pb = c % 2

                                def mmz(e, c=c, pb=pb):
                                    for cc in range(8):
                                        ins = e.matmul(qC[:, pb, 0:256], lhsT=hT[:, cc, c * 128:(c + 1) * 128], rhs=wz[:, cc, :],
                                                       start=(cc == 0), stop=(cc == 7))
                                    return ins
                                op("pe", mmz, reads=["wz"], writes=[("bk_qC", pb)], cost=1.2)
                                op("act", lambda e, c=c, pb=pb: e.activation(out=szall[:, c, :], in_=qC[:, pb, 0:256], func=AF.Silu),
                                   reads=[], writes=[("szall", c), ("bk_qC", pb)])

                            def do_transposes(c, pb):
                                cs = slice(c * 128, (c + 1) * 128)
                                blk = c // 4

                                def tr(e):
                                    e.matmul(qD[:, pb, 128:256], lhsT=xc0[:, cs], rhs=ident_b[:], start=True, stop=True)
                                    e.matmul(qD[:, pb, 256:384], lhsT=xc1[:, cs], rhs=ident_b[:], start=True, stop=True)
                                    return e.matmul(qD[:, pb, 384:512], lhsT=xcB[:, cs], rhs=ident_b[:], start=True, stop=True)
                                op("pe", tr, reads=[("xc", "x0", blk), ("xc", "x1", blk), ("xc", "B", blk)],
                                   writes=[("bk_qD", pb)], cost=0.4)
                                op("act", lambda e: e.activation(out=xb[pb][:], in_=qD[:, pb, 128:512], func=AF.Copy),
                                   reads=[], writes=[("xb", pb), ("bk_qD", pb)])

                            op("pool", lambda e: e.memset(Hb[:], 0.0), writes=["Hb"])
                            op("pool", lambda e: e.memset(Hf[:], 0.0), writes=["Hf"])
                            op("pool", lambda e: e.memset(Hfbs[0][:], 0.0), writes=[("Hfb", 0)])

                            for c in range(NT - 1, -1, -1):
                                pb = c % 2
                                xw = xws[pb]
                                do_transposes(c, pb)
                                op("pool", lambda e, c=c, pb=pb, xw=xw: e.tensor_tensor(
                                    out=x4(xw[:]), in0=x4(xb[pb][:, 0:256]), in1=b4(cfall[:, c, 4:8]), op=ALU.mult),
                                   reads=["cfall1", ("xb", pb)], writes=[("xw", pb)])
                                op("pe", lambda e, pb=pb, xw=xw: e.matmul(qC[:, pb, 0:256], lhsT=xb[pb][:, 256:384], rhs=xw[:],
                                                                          start=True, stop=True),
                                   reads=[("xw", pb), ("xb", pb)], writes=[("bk_qC", pb)])
                                op("act", lambda e, c=c: e.activation(out=hbsn[:, c, :], in_=Hb[:], func=AF.Copy),
                                   reads=["Hb"], writes=[("hbsn", c)])
                                op("dve", lambda e, c=c: e.tensor_tensor(out=x4(Hb[:]), in0=x4(Hb[:]), in1=b4(evall[:, c, 20:24]),
                                                                         op=ALU.mult),
                                   reads=["Hb", "evall"], writes=["Hb"])
                                op("dve", lambda e, pb=pb: e.tensor_tensor(out=Hb[:], in0=Hb[:], in1=qC[:, pb, 0:256], op=ALU.add),
                                   reads=["Hb"], writes=["Hb", ("bk_qC", pb)])

                            for c in range(NT):
                                pb = c % 2
                                cs = slice(c * 128, (c + 1) * 128)
                                blk = c // 4
                                xw, xdf, xdb, xD = xws[pb], xdfs[pb], xdbs[pb], xDs[pb]
                                t1, t2, yz = t1s[pb], t2s[pb], yzs[pb]
                                ynb_ = ynTgs[blk % 2]
                                do_transposes(c, pb)
                                op("pe", lambda e, cs=cs, pb=pb: e.matmul(qD[:, pb, 0:128], lhsT=xcB[:, cs], rhs=xcC[:, cs],
                                                                          start=True, stop=True),
                                   reads=[("xc", "B", blk), ("xc", "C", blk)], writes=[("bk_qD", pb)])
                                op("pool", lambda e, c=c: e.tensor_tensor(
                                    out=rDf[:], in0=bc(dt_tm[:, c, af].unsqueeze(2), [128, 4, 128]),
                                    in1=bc(tri_b[:, 0:1, :], [128, 4, 128]), op=ALU.mult),
                                   reads=[("dt_tm", blk)], writes=["rDf"], cost=0.8)
                                op("pool", lambda e, c=c: e.tensor_tensor(
                                    out=rDb[:], in0=bc(dt_tm[:, c, ab].unsqueeze(2), [128, 4, 128]),
                                    in1=bc(tri_b[:, 2:3, :], [128, 4, 128]), op=ALU.mult),
                                   reads=[("dt_tm", blk)], writes=["rDb"], cost=0.8)

                                def segf(e):
                                    e.matmul(qA[:, 0, :], lhsT=tri_b[:, 1, :], rhs=rDf[:].rearrange("p h l -> p (h l)"),
                                             start=True, stop=False)
                                    return e.matmul(qA[:, 0, :], lhsT=ident_b[:], rhs=neg_b[:, 0, :], start=False, stop=True)

                                def segb(e):
                                    e.matmul(qA[:, 1, :], lhsT=tri_b[:, 3, :], rhs=rDb[:].rearrange("p h l -> p (h l)"),
                                             start=True, stop=False)
                                    return e.matmul(qA[:, 1, :], lhsT=ident_b[:], rhs=neg_b[:, 1, :], start=False, stop=True)
                                op("pe", segf, reads=["rDf"], writes=[("qA", 0)], cost=0.6)
                                op("pe", segb, reads=["rDb"], writes=[("qA", 1)], cost=0.6)
                                op("act", lambda e: e.activation(out=Df[:].rearrange("p h l -> p (h l)"), in_=qA[:, 0, :], func=AF.Exp),
                                   reads=[], writes=["Df", ("qA", 0)], cost=0.6)
                                op("act", lambda e: e.activation(out=Db[:].rearrange("p h l -> p (h l)"), in_=qA[:, 1, :], func=AF.Exp),
                                   reads=[], writes=["Db", ("qA", 1)], cost=0.6)
                                op("dve", lambda e, pb=pb: e.tensor_tensor(
                                    out=Mf[:], in0=Df[:], in1=bc(qD[:, pb, 0:128].unsqueeze(1), [128, 4, 128]), op=ALU.mult),
                                   reads=["Df"], writes=["Mf", ("bk_qD", pb)], cost=0.7)
                                op("dve", lambda e, pb=pb: e.tensor_tensor(
                                    out=Mb[:], in0=Db[:], in1=bc(qD[:, pb, 0:128].unsqueeze(1), [128, 4, 128]), op=ALU.mult),
                                   reads=["Db"], writes=["Mb", ("bk_qD", pb)], cost=0.7)
                                op("dve", lambda e, c=c, pb=pb, xdf=xdf: e.tensor_tensor(
                                    out=x4(xdf[:]), in0=x4(xb[pb][:, 0:256]), in1=b4(dt_tm[:, c, gf]), op=ALU.mult),
                                   reads=[("xb", pb), ("dt_tm", blk)], writes=[("xdf", pb)])
                                op("dve", lambda e, c=c, pb=pb, xdb=xdb: e.tensor_tensor(
                                    out=x4(xdb[:]), in0=x4(xb[pb][:, 0:256]), in1=b4(dt_tm[:, c, gb]), op=ALU.mult),
                                   reads=[("xb", pb), ("dt_tm", blk)], writes=[("xdb", pb)])
                                op("pool", lambda e, pb=pb, xD=xD: e.tensor_tensor(
                                    out=x4(xD[:]), in0=x4(xb[pb][:, 0:256]), in1=b4(dsk[:, gf]), op=ALU.mult),
                                   reads=[("xb", pb)], writes=[("xD", pb)])
                                op("pool", lambda e, c=c, pb=pb, xw=xw: e.tensor_tensor(
                                    out=x4(xw[:]), in0=x4(xb[pb][:, 0:256]), in1=b4(cfall[:, c, 0:4]), op=ALU.mult),
                                   reads=[("xb", pb), "cfall0"], writes=[("xw", pb)])

                                def ydiag(e, pb=pb, xdf=xdf, xdb=xdb, xD=xD):
                                    for h in range(4):
                                        hs = slice(h * 64, (h + 1) * 64)
                                        o_ = qC[:, pb, 256 + h * 64:256 + (h + 1) * 64]
                                        e.matmul(o_, lhsT=Mf[:, h, :], rhs=xdf[:, hs], start=True, stop=False)
                                        e.matmul(o_, lhsT=Mb[:, h, :], rhs=xdb[:, hs], start=False, stop=False)
                                        ins = e.matmul(o_, lhsT=ident_b[:], rhs=xD[:, hs], start=False, stop=True)
                                    return ins
                                op("pe", ydiag, reads=["Mf", "Mb", ("xdf", pb), ("xdb", pb), ("xD", pb)], writes=[("bk_qC", pb)], cost=1.0)
                                op("pe", lambda e, pb=pb, xw=xw: e.matmul(qC[:, pb, 0:256], lhsT=xb[pb][:, 256:384], rhs=xw[:],
                                                                          start=True, stop=True),
                                   reads=[("xw", pb), ("xb", pb)], writes=[("bk_qC", pb)])

                                def yoff(e, c=c, cs=cs):
                                    e.matmul(qO[:, 0:256], lhsT=xcC[:, cs], rhs=Hfbs[c % 2][:], start=True, stop=True)
                                    return e.matmul(qO[:, 256:512], lhsT=xcC[:, cs], rhs=hbsn[:, c, :], start=True, stop=True)
                                op("pe", yoff, reads=[("xc", "C", blk), ("Hfb", c % 2), ("hbsn", c)], writes=["qO"], cost=0.45)
                                op("dve", lambda e, c=c: e.tensor_tensor(out=x4(Hf[:]), in0=x4(Hf[:]), in1=b4(evall[:, c, 8:12]),
                                                                         op=ALU.mult),
                                   reads=["Hf", "evall"], writes=["Hf"])
                                op("dve", lambda e, pb=pb: e.tensor_tensor(out=Hf[:], in0=Hf[:], in1=qC[:, pb, 0:256], op=ALU.add),
                                   reads=["Hf"], writes=["Hf", ("bk_qC", pb)])
                                op("act", lambda e, c=c: e.activation(out=Hfbs[(c + 1) % 2][:], in_=Hf[:], func=AF.Copy),
                                   reads=["Hf"], writes=[("Hfb", (c + 1) % 2)])
                                op("dve", lambda e, c=c, t1=t1: e.tensor_tensor(out=x4(t1[:]), in0=x4(qO[:, 0:256]),
                                                                                in1=b4(evall[:, c, 0:4]), op=ALU.mult),
                                   reads=["evall"], writes=[("t1", pb), "qO"])
                                op("dve", lambda e, c=c, t2=t2: e.tensor_tensor(out=x4(t2[:]), in0=x4(qO[:, 256:512]),
                                                                                in1=b4(evall[:, c, 12:16]), op=ALU.mult),
                                   reads=["evall"], writes=[("t2", pb), "qO"])
                                op("pool", lambda e, t1=t1, t2=t2: e.tensor_tensor(out=t1[:], in0=t1[:], in1=t2[:], op=ALU.add),
                                   reads=[("t2", pb)], writes=[("t1", pb)])
                                op("dve", lambda e, pb=pb, t1=t1: e.tensor_tensor(out=t1[:], in0=t1[:], in1=qC[:, pb, 256:512],
                                                                                  op=ALU.add),
                                   reads=[], writes=[("t1", pb), ("bk_qC", pb)])
                                op("dve", lambda e, c=c, t1=t1, yz=yz: e.tensor_tensor(out=yz[:], in0=t1[:], in1=szall[:, c, :],
                                                                                       op=ALU.mult),
                                   reads=[("t1", pb), ("szall", c)], writes=[("yz", pb)])
                                op("act", lambda e, yz=yz: e.activation(out=sqj[:], in_=yz[:], func=AF.Square, accum_out=ssq[:]),
                                   reads=[("yz", pb)], writes=["sqj", "ssq"])
                                op("act", lambda e: e.activation(out=lnv[:], in_=ssq[:], func=AF.Ln, scale=1.0 / 256, bias=EPS),
                                   reads=["ssq"], writes=["lnv"], cost=0.25)
                                op("act", lambda e: e.activation(out=rsv[:], in_=lnv[:], func=AF.Exp, scale=-0.5),
                                   reads=["lnv"], writes=["rsv"], cost=0.25)
                                op("pool", lambda e, yz=yz, t2=t2: e.tensor_tensor(out=t2[:], in0=yz[:], in1=ssdn[:], op=ALU.mult),
                                   reads=[("yz", pb), "ssdn"], writes=[("t2", pb)])
                                op("act", lambda e, t2=t2: e.activation(out=yn[:], in_=t2[:], func=AF.Copy, scale=rsv[:]),
                                   reads=[("t2", pb), "rsv"], writes=["yn"])

                                def try_(e, pb=pb):
                                    e.matmul(qC[:, pb, 256:384], lhsT=yn[:, 0:128], rhs=ident_b[:], start=True, stop=True)
                                    return e.matmul(qC[:, pb, 384:512], lhsT=yn[:, 128:256], rhs=ident_b[:], start=True, stop=True)
                                op("pe", try_, reads=["yn"], writes=[("bk_qC", pb)])
                                op("act", lambda e, c=c, pb=pb, ynb_=ynb_: e.activation(
                                    out=ynb_[:, :, (c % 4) * 128:(c % 4 + 1) * 128],
                                    in_=qC[:, pb, 256:512].rearrange("p (a b) -> p a b", a=2), func=AF.Copy),
                                   reads=[], writes=[("ynTg", blk % 2, c % 4), ("bk_qC", pb)])
                                if c % 4 == 3:
                                    dma(ynT[g * 256:(g + 1) * 256, blk * 512:(blk + 1) * 512].rearrange("(k p) s -> p k s", p=128),
                                        ynb_[:], reads=[("ynTg", blk % 2, q_) for q_ in range(4)], writes=[("ynT", g, blk)])
                        for g_ in range(int(os.environ.get('DBG_NG', '8'))):
                            do_group(g_)
                        S_.flush()
                        if upto == 'p2b':
                            raise _Stop(nc)

            with _Alloc(nc) as A7:
                Usb = A7.sb("Usb", [128, NT, D], BF16)
                YTa = A7.sb("YTa", [128, 8, S], BF16)
                tb0 = A7.sb("tb0", [128, NT, 2, 128], BF16)
                tb1 = A7.sb("tb1", [128, NT, 2, 128], BF16)
                cs128 = A7.sb("cs128", [128, 384], BF16)
                antiid = A7.sb("antiid", [128, 128], BF16)
                vwTR = A7.sb("vwTR", [128, 16, 128], BF16)
                vw = A7.sb("vw", [128, 2, D], BF16)
                vwT = A7.sb("vwT", [128, 16, 128], BF16)
                fV = A7.ps("fV", [128, 2, 2, 512], F32)
                fT = A7.ps("fT", [128, 8, 128], F32)
                fY = A7.ps("fY", [128, 2, 512], F32)
                tb = [tb0, tb1]
                dma(cs128[:], cs128_d[S], writes=["cs128"])
                dma(antiid[:], antiid_d, writes=["antiid"])
                for t in range(NT):
                    dma(Usb[:, t, :], Utm[t * 128:(t + 1) * 128, :], writes=[("Usb", t)])
                for kt in range(NT // 2 + 1):
                    b = kt % 2
                    dma(tb[b][:], tab_d[S][kt], writes=[("tb", b)])

                    def mmf(e, b=b):
                        for cs_ in range(2):
                            for hf in range(2):
                                for t in range(NT):
                                    ins = e.matmul(fV[:, cs_, hf, :], lhsT=tb[b][:, t, cs_, :],
                                                   rhs=Usb[:, t, hf * 512:(hf + 1) * 512], start=(t == 0), stop=(t == NT - 1))
                        return ins
                    op("pe", mmf, reads=[("tb", b)] + [("Usb", t) for t in range(NT)], writes=["fV"])
                    op("act", lambda e: e.activation(out=vw[:, 0, :], in_=fV[:, 0].rearrange("p a b -> p (a b)"), func=AF.Copy),
                       reads=["fV"], writes=["vw0"])
                    op("dve", lambda e: e.tensor_copy(out=vw[:, 1, :], in_=fV[:, 1].rearrange("p a b -> p (a b)")),
                       reads=["fV"], writes=["vw1"])

                    def trf0(e):
                        for q in range(8):
                            ins = e.matmul(fT[:, q, :], lhsT=vw[:, 0, q * 128:(q + 1) * 128], rhs=ident_b[:],
                                           start=True, stop=True)
                        return ins

                    def trf1(e):
                        for q in range(8):
                            ins = e.matmul(fT[:, q, :], lhsT=vw[:, 1, q * 128:(q + 1) * 128], rhs=ident_b[:],
                                           start=True, stop=True)
                        return ins
                    op("pe", trf0, reads=["vw0"], writes=["fT"])
                    op("act", lambda e: e.activation(out=vwT[:, 0:8, :], in_=fT[:], func=AF.Copy),
                       reads=["fT"], writes=["vwT0"])
                    op("pe", trf1, reads=["vw1"], writes=["fT"])
                    op("dve", lambda e: e.tensor_copy(out=vwT[:, 8:16, :], in_=fT[:]),
                       reads=["fT"], writes=["vwT1"])

                    def mmy(e):
                        for g in range(8):
                            o_ = fY[:, g // 4, (g % 4) * 128:(g % 4 + 1) * 128]
                            e.matmul(o_, lhsT=cs128[:, 0:128], rhs=vwT[:, g, :], start=True, stop=False)
                            ins = e.matmul(o_, lhsT=cs128[:, 128:256], rhs=vwT[:, 8 + g, :], start=False, stop=True)
                        return ins
                    op("pe", mmy, reads=["vwT0", "vwT1", "cs128"], writes=["fY"])
                    op("act", lambda e, kt=kt: e.activation(
                        out=YTa[:, :, kt * 128:(kt + 1) * 128],
                        in_=fY[:].rearrange("p a (g k) -> p (a g) k", g=4), func=AF.Copy),
                       reads=["fY"], writes=[("YTa", kt)])
                    if kt < NT // 2:
                        def trr0(e):
                            for q in range(8):
                                ins = e.matmul(fT[:, q, :], lhsT=vw[:, 0, q * 128:(q + 1) * 128], rhs=antiid[:],
                                               start=True, stop=True)
                            return ins

                        def trr1(e):
                            for q in range(8):
                                ins = e.matmul(fT[:, q, :], lhsT=vw[:, 1, q * 128:(q + 1) * 128], rhs=antiid[:],
                                               start=True, stop=True)
                            return ins
                        op("pe", trr0, reads=["vw0", "antiid"], writes=["fT"])
                        op("act", lambda e: e.activation(out=vwTR[:, 0:8, :], in_=fT[:], func=AF.Copy),
                           reads=["fT"], writes=["vwTR0"])
                        op("pe", trr1, reads=["vw1", "antiid"], writes=["fT"])
                        op("dve", lambda e: e.tensor_copy(out=vwTR[:, 8:16, :], in_=fT[:]),
                           reads=["fT"], writes=["vwTR1"])

                        def mmy2(e):
                            for g in range(8):
                                o_ = fY[:, g // 4, (g % 4) * 128:(g % 4 + 1) * 128]
                                e.matmul(o_, lhsT=cs128[:, 0:128], rhs=vwTR[:, g, :], start=True, stop=False)
                                ins = e.matmul(o_, lhsT=cs128[:, 256:384], rhs=vwTR[:, 8 + g, :], start=False, stop=True)
                            return ins
                        op("pe", mmy2, reads=["vwTR0", "vwTR1", "cs128"], writes=["fY"])
                        st_ = S - kt * 128 - 127
                        ncol = 127 if kt == 0 else 128
                        op("act", lambda e, st_=st_, ncol=ncol: e.activation(
                            out=YTa[:, :, st_:st_ + ncol],
                            in_=fY[:].rearrange("p a (g k) -> p (a g) k", g=4)[:, :, 0:ncol], func=AF.Copy),
                           reads=["fY"], writes=["YTa_m"])
                dma(YT[:, 0:S].rearrange("(g p) s -> p g s", p=128), YTa[:],
                    reads=[("YTa", kt) for kt in range(NT // 2 + 1)] + ["YTa_m"], writes=["YT"])
                S_.flush()
                if upto == 'pf':
                    raise _Stop(nc)

            with _Alloc(nc) as A8:
                wso = A8.sb("wso", [128, 16, D], BF16)
                wfo = A8.sb("wfo", [128, 8, D], BF16)
                wo = A8.sb("wo", [128, 8, D], BF16)
                ynb0 = A8.sb("ynb0", [128, 16, 512], BF16)
                ynb1 = A8.sb("ynb1", [128, 16, 512], BF16)
                ytb0 = A8.sb("ytb0", [128, 8, 512], BF16)
                ytb1 = A8.sb("ytb1", [128, 8, 512], BF16)
                gtb0 = A8.sb("gtb0", [128, 16, 512], BF16)
                gtb1 = A8.sb("gtb1", [128, 16, 512], BF16)
                mT = A8.sb("mT", [128, 8, 512], BF16)
                m1 = A8.sb("m1", [128, 512], F32)
                m2 = A8.sb("m2", [128, 512], F32)
                xa0 = A8.sb("xa0", [128, D], F32)
                xa1 = A8.sb("xa1", [128, D], F32)
                xo0 = A8.sb("xo0", [128, D], F32)
                xo1 = A8.sb("xo1", [128, D], F32)
                gA = A8.ps("gA", [128, 2, 512], F32)
                gF = A8.ps("gF", [128, 2, 512], F32)
                gO = A8.ps("gO", [128, 2, 2, 512], F32)
                ynb = [ynb0, ynb1]; ytb = [ytb0, ytb1]; gtb = [gtb0, gtb1]; xa = [xa0, xa1]; xo = [xo0, xo1]
                for m_ in range(8):
                    dma(wso[:, :, m_ * 128:(m_ + 1) * 128], w_so_l[:, :, m_ * 128:(m_ + 1) * 128], writes=[("wso", m_)], eng="pool")
                    dma(wfo[:, :, m_ * 128:(m_ + 1) * 128], w_fo_l[:, :, m_ * 128:(m_ + 1) * 128], writes=[("wfo", m_)], eng="pool")
                dma(wo[:], w_o_l, writes=["wo"], eng="pool")
                nt_ = 0
                for blk in range(NB):
                    b = blk % 2
                    bs = slice(blk * 512, (blk + 1) * 512)
                    dma(ynb[b][:], ynT[:, bs].rearrange("(k p) s -> p k s", p=128), writes=[("ynb", b)])
                    dma(ytb[b][:], YT[:, bs].rearrange("(k p) s -> p k s", p=128), writes=[("ytb", b)])
                    dma(gtb[b][:], gatesT[:, bs].rearrange("(k p) s -> p k s", p=128), writes=[("gtb", b)])
                    for m in range(8):
                        pb = m % 2

                        def mma(e, b=b, m=m, pb=pb):
                            for cc in range(16):
                                ins = e.matmul(gA[:, pb, :], lhsT=wso[:, cc, m * 128:(m + 1) * 128], rhs=ynb[b][:, cc, :],
                                               start=(cc == 0), stop=(cc == 15))
                            return ins

                        def mmf2(e, b=b, m=m, pb=pb):
                            for cc in range(8):
                                ins = e.matmul(gF[:, pb, :], lhsT=wfo[:, cc, m * 128:(m + 1) * 128], rhs=ytb[b][:, cc, :],
                                               start=(cc == 0), stop=(cc == 7))
                            return ins
                        op("pe", mma, reads=[("wso", m), ("ynb", b)], writes=[("gA", pb)])
                        op("pe", mmf2, reads=[("wfo", m), ("ytb", b)], writes=[("gF", pb)])
                        op("dve", lambda e, b=b, m=m, pb=pb: e.tensor_tensor(out=m1[:], in0=gtb[b][:, m, :], in1=gA[:, pb, :],
                                                                             op=ALU.mult),
                           reads=[("gA", pb), ("gtb", b)], writes=["m1"])
                        op("act", lambda e, m=m, pb=pb: e.activation(out=m2[:], in_=gF[:, pb, :], func=AF.Identity,
                                                                     bias=bfo[:, m:m + 1]),
                           reads=[("gF", pb)], writes=["m2"])
                        op("pool", lambda e, b=b, m=m: e.tensor_tensor(out=m2[:], in0=m2[:], in1=gtb[b][:, 8 + m, :], op=ALU.mult),
                           reads=["m2", ("gtb", b)], writes=["m2"])
                        op("dve", lambda e, m=m: e.tensor_tensor(out=mT[:, m, :], in0=m1[:], in1=m2[:], op=ALU.add),
                           reads=["m1", "m2"], writes=[("mT", m)])
                    for tt in range(4):
                        t = blk * 4 + tt
                        xbuf = nt_ % 2
                        nt_ += 1
                        dma(xa[xbuf][:], x_seq[t * 128:(t + 1) * 128, :], writes=[("xa", xbuf)])

                        def mmo(e, tt=tt, xbuf=xbuf):
                            for hf in range(2):
                                for cc in range(8):
                                    ins = e.matmul(gO[:, xbuf, hf, :], lhsT=mT[:, cc, tt * 128:(tt + 1) * 128],
                                                   rhs=wo[:, cc, hf * 512:(hf + 1) * 512], start=(cc == 0), stop=(cc == 7))
                            return ins
                        op("pe", mmo, reads=["wo"] + [("mT", m) for m in range(8)], writes=[("gO", xbuf)])
                        op("dve", lambda e, xbuf=xbuf: e.tensor_tensor(out=xo[xbuf][:], in0=xa[xbuf][:],
                                                                       in1=gO[:, xbuf].rearrange("p a b -> p (a b)"), op=ALU.add),
                           reads=[("xa", xbuf), ("gO", xbuf)], writes=[("xo", xbuf)])
                        dma(x2d[t * 128:(t + 1) * 128, :], xo[xbuf][:], reads=[("xo", xbuf)], writes=[("x2d", t)])
                S_.flush()
                if upto == 'p3a':
                    raise _Stop(nc)

            with _Alloc(nc) as A9:
                wgu = A9.sb("wgu", [128, 8, 2 * DFF], BF16)
                wd = A9.sb("wd", [128, 22, D], BF16)
                gfin = A9.sb("gfin", [128, D], F32)
                xq0 = A9.sb("xq0", [128, D], F32)
                xq1 = A9.sb("xq1", [128, D], F32)
                xm0 = A9.sb("xm0", [128, D], BF16)
                xm1 = A9.sb("xm1", [128, D], BF16)
                fs0 = A9.sb("fs0", [128, 1], F32)
                fs1 = A9.sb("fs1", [128, 1], F32)
                fd0 = A9.sb("fd0", [128, 1], F32)
                fd1 = A9.sb("fd1", [128, 1], F32)
                fr0 = A9.sb("fr0", [128, 1], F32)
                fr1 = A9.sb("fr1", [128, 1], F32)
                h2T = A9.sb("h2T", [128, 8, 256], BF16)
                aT = A9.sb("aT", [128, 22, 256], BF16)
                sg = A9.sb("sg", [128, 256], F32)
                sg1 = A9.sb("sg1", [128, 256], F32)
                sgs = [sg, sg1]
                x3 = A9.sb("x3", [128, D], F32)
                sq3 = A9.sb("sq3", [128, D], BF16)
                s3 = A9.sb("s3", [128, 1], F32)
                d3 = A9.sb("d3", [128, 1], F32)
                r3 = A9.sb("r3", [128, 1], F32)
                ob0 = A9.sb("ob0", [128, D], F32)
                ob1 = A9.sb("ob1", [128, D], F32)
                hP = A9.ps("hP", [128, 2, 8, 128], BF16)
                hG = A9.ps("hG", [128, 4, 512], F32)
                hD = A9.ps("hD", [128, 1, 2, 512], F32)
                xq = [xq0, xq1]; xm = [xm0, xm1]; fs = [fs0, fs1]; fd = [fd0, fd1]; fr = [fr0, fr1]; ob = [ob0, ob1]
                for j_ in range(11):
                    c0_ = j_ * 256
                    dma(wgu[:, :, c0_:c0_ + 256], w_gu_l[:, :, c0_:c0_ + 256], writes=[("wgu", j_, 0)], eng="pool")
                    dma(wgu[:, :, DFF + c0_:DFF + c0_ + 256], w_gu_l[:, :, DFF + c0_:DFF + c0_ + 256], writes=[("wgu", j_, 1)],
                        eng="pool")
                dma(wd[:], w_d_l, writes=["wd"], eng="pool")
                dma(gfin[:], gfin_bc, writes=["gfin"])
                for blk in range(S // 256):
                    for tt in range(2):
                        t = blk * 2 + tt
                        dma(xq[tt][:], x2d[t * 128:(t + 1) * 128, :], writes=[("xq", tt)])
                        rms_to_T("p3", tt, xq[tt][:], xm[tt], fs[tt], fd[tt], fr[tt], hP[:, tt], gffn,
                                 h2T[:, :, tt * 128:(tt + 1) * 128], [("xq", tt)], ("h2T", tt))
                    for i in range(22):
                        q = i % 2

                        def mmg(e, i=i, q=q):
                            for cc in range(8):
                                e.matmul(hG[:, 2 * q, 0:256], lhsT=wgu[:, cc, i * 128:(i + 1) * 128], rhs=h2T[:, cc, :],
                                         start=(cc == 0), stop=(cc == 7))
                            for cc in range(8):
                                ins = e.matmul(hG[:, 2 * q + 1, 0:256], lhsT=wgu[:, cc, DFF + i * 128:DFF + (i + 1) * 128],
                                               rhs=h2T[:, cc, :], start=(cc == 0), stop=(cc == 7))
                            return ins
                        op("pe", mmg, reads=[("wgu", i // 2, 0), ("wgu", i // 2, 1), ("h2T", 0), ("h2T", 1)], writes=[("hGg", q), ("hGu", q)])
                        op("act", lambda e, q=q: e.activation(out=sgs[q][:], in_=hG[:, 2 * q, 0:256], func=AF.Silu),
                           reads=[("hGg", q)], writes=[("sg", q)])
                        op("dve", lambda e, i=i, q=q: e.tensor_tensor(out=aT[:, i, :], in0=sgs[q][:], in1=hG[:, 2 * q + 1, 0:256],
                                                                      op=ALU.mult),
                           reads=[("sg", q), ("hGu", q)], writes=[("aT", i)])
                    for tt in range(2):
                        t = blk * 2 + tt

                        def mmd2(e, tt=tt):
                            for hf in range(2):
                                for i in range(22):
                                    ins = e.matmul(hD[:, 0, hf, :], lhsT=aT[:, i, tt * 128:(tt + 1) * 128],
                                                   rhs=wd[:, i, hf * 512:(hf + 1) * 512], start=(i == 0), stop=(i == 21))
                            return ins
                        op("pe", mmd2, reads=["wd"] + [("aT", i) for i in range(22)], writes=["hD"])
                        op("dve", lambda e, tt=tt: e.tensor_tensor(out=x3[:], in0=xq[tt][:],
                                                                   in1=hD[:, 0].rearrange("p a b -> p (a b)"), op=ALU.add),
                           reads=[("xq", tt), "hD"], writes=["x3"])
                        op("act", lambda e: e.activation(out=sq3[:], in_=x3[:], func=AF.Square, accum_out=s3[:]),
                           reads=["x3"], writes=["sq3", "s3"])
                        op("act", lambda e: e.activation(out=d3[:], in_=s3[:], func=AF.Sqrt, scale=1.0 / D, bias=EPS),
                           reads=["s3"], writes=["d3"])
                        op("dve", lambda e: e.reciprocal(out=r3[:], in_=d3[:]), reads=["d3"], writes=["r3"])
                        op("act", lambda e, tt=tt: e.activation(out=ob[tt][:], in_=x3[:], func=AF.Copy, scale=r3[:]),
                           reads=["x3", "r3"], writes=[("ob", tt)])
                        op("dve", lambda e, tt=tt: e.tensor_tensor(out=ob[tt][:], in0=ob[tt][:], in1=gfin[:], op=ALU.mult),
                           reads=[("ob", tt), "gfin"], writes=[("ob", tt)])
                        dma(y_seq[t * 128:(t + 1) * 128, :], ob[tt][:], reads=[("ob", tt)], writes=[("y", t)])
                S_.flush()
                if upto == 'p3b':
                    raise _Stop(nc)
            s0 += S
    return nc


def _tile_w(w):
    k, n = w.shape
    return np.ascontiguousarray(w.reshape(k // 128, 128, n).transpose(1, 0, 2))


def _col(v):
    return np.ascontiguousarray(v.reshape(-1, 128).T)


def _bcast(v):
    return np.ascontiguousarray(np.broadcast_to(v.reshape(1, -1), (128, v.size)))


_CONST_CACHE = {}


def _consts(uniqS):
    key = tuple(uniqS)
    if key in _CONST_CACHE:
        return _CONST_CACHE[key]
    bf = ml_dtypes.bfloat16
    j = np.arange(128)[:, None]
    l = np.arange(128)[None, :]
    Lle = (j <= l).astype(np.float32)
    Lgt = (j > l).astype(np.float32)
    Lge = (j >= l).astype(np.float32)
    Llt = (j < l).astype(np.float32)
    ones = np.ones((128, 128), np.float32)
    c = {}
    c["ident_b"] = np.eye(128, dtype=np.float32).astype(bf)
    c["ident_f"] = np.eye(128, dtype=np.float32)
    c["antiid_b"] = np.ascontiguousarray(np.eye(128, dtype=np.float32)[:, ::-1]).astype(bf)
    c["tri_f"] = np.ascontiguousarray(np.stack([Lle, Lgt, Lge, Llt, ones], axis=1))
    c["tri_b"] = np.ascontiguousarray(np.stack([Lle, Lgt, Lge, Llt], axis=1)).astype(bf)
    negf = NEG * (l < j).astype(np.float32)
    negb = NEG * (l > j).astype(np.float32)
    c["neg_b"] = np.ascontiguousarray(np.stack([np.tile(negf, (1, 4)), np.tile(negb, (1, 4))], axis=1)).astype(bf)
    for S in uniqS:
        NT = S // 128
        sc = 1.0 / math.sqrt(S * 128.0)
        ang = 2.0 * np.pi * (np.arange(128)[:, None] * np.arange(128)[None, :] % 128) / 128.0
        c["cs128_%d" % S] = np.concatenate([np.cos(ang) * sc, -np.sin(ang) * sc, np.sin(ang) * sc], axis=1).astype(np.float32).astype(bf)
        n = np.arange(S, dtype=np.int64)
        prod = (n[:, None] * n[None, :]) % S
        a = 2.0 * np.pi * prod.astype(np.float64) / S
        cs = np.stack([np.cos(a), np.sin(a)], axis=0).astype(np.float32)
        tab = cs.reshape(2, NT, 128, NT, 128).transpose(3, 2, 1, 0, 4)
        c["tab_%d" % S] = np.ascontiguousarray(tab).astype(bf)
    _CONST_CACHE[key] = c
    return c


def _shared_inputs(norm_mix, w_in, conv_w, conv_b, dt_bias_f, dt_bias_b, a_log_f, a_log_b, d_skip, ssd_norm,
                   w_ssd_out, w_fourier_out, b_fourier_out, w_out, norm_ffn, w_gate_up, w_down, norm_final, uniqS):
    f = lambda a: np.asarray(a, dtype=np.float32)
    m = {}
    m["w_in_l"] = _tile_w(f(w_in)[0])
    m["w_so_l"] = _tile_w(f(w_ssd_out)[0])
    m["w_fo_l"] = _tile_w(f(w_fourier_out)[0])
    m["w_o_l"] = _tile_w(f(w_out)[0])
    m["w_gu_l"] = _tile_w(f(w_gate_up)[0])
    m["w_d_l"] = _tile_w(f(w_down)[0])
    m["convw_l"] = np.ascontiguousarray(f(conv_w)[0].T.reshape(32, 128, 7).transpose(1, 0, 2))
    m["convb_l"] = _col(f(conv_b)[0])
    dtb = np.concatenate([f(dt_bias_f)[0], f(dt_bias_b)[0]])
    m["dtb_col"] = np.ascontiguousarray(np.concatenate([dtb, dtb]).reshape(128, 1))
    al = np.concatenate([f(a_log_f)[0], f(a_log_b)[0]])
    m["alog_col"] = np.ascontiguousarray(np.concatenate([al, al]).reshape(128, 1))
    m["dskip_bc"] = _bcast(f(d_skip)[0])
    m["ssdn_bc"] = _bcast(f(ssd_norm)[0])
    m["gmix_col"] = _col(f(norm_mix)[0])
    m["gffn_col"] = _col(f(norm_ffn)[0])
    m["gfin_bc"] = _bcast(f(norm_final))
    m["bfo_col"] = _col(f(b_fourier_out)[0])
    m.update(_consts(uniqS))
    return m


def kernel(x_prompt, x_sample, norm_mix, w_in, conv_w, conv_b, dt_bias_f, dt_bias_b, a_log_f, a_log_b,
           d_skip, ssd_norm, w_ssd_out, w_fourier_out, b_fourier_out, w_out, norm_ffn, w_gate_up, w_down,
           norm_final):
    xp = np.asarray(x_prompt, dtype=np.float32)
    xs = np.asarray(x_sample, dtype=np.float32)
    n = 8
    Sp, Ss = xp.shape[1], xs.shape[1]
    seq_lens = [Sp, Ss, Ss]
    shared = _shared_inputs(norm_mix, w_in, conv_w, conv_b, dt_bias_f, dt_bias_b, a_log_f, a_log_b, d_skip,
                            ssd_norm, w_ssd_out, w_fourier_out, b_fourier_out, w_out, norm_ffn, w_gate_up,
                            w_down, norm_final, sorted(set(seq_lens)))
    nc = build(seq_lens)
    in_maps = []
    for i in range(n):
        m = dict(shared)
        m["x_all"] = np.ascontiguousarray(np.concatenate([xp[i], xs[2 * i], xs[2 * i + 1]], axis=0))
        in_maps.append(m)
    res = run_bass_kernel_spmd(nc, in_maps, core_ids=list(range(n)))
    yp = np.empty_like(xp)
    ys = np.empty_like(xs)
    for i in range(n):
        y = np.asarray(res.results[i]["y_all"], dtype=np.float32)
        yp[i] = y[0:Sp]
        ys[2 * i] = y[Sp:Sp + Ss]
        ys[2 * i + 1] = y[Sp + Ss:Sp + 2 * Ss]
    return (yp, ys)
```

```python
import math
import os
from contextlib import ExitStack
import numpy as np
import ml_dtypes
import concourse.bass as bass
import concourse.mybir as mybir
from concourse.bass_utils import run_bass_kernel_spmd

F32 = mybir.dt.float32
BF16 = mybir.dt.bfloat16
AF = mybir.ActivationFunctionType
ALU = mybir.AluOpType

D = 1024
DI = 2048
NIN = 9280
DFF = 2816
EPS = 1e-5
OFF_Z, OFF_X, OFF_B, OFF_C, OFF_DT, OFF_U, OFF_G = 0, 2048, 4096, 5120, 6144, 6208, 7232
NEG = -30000.0


class _Op:
    __slots__ = ("eng", "fn", "deps", "is_dma", "sem", "val", "lane_prev", "cost", "cost_given")


class Sched:
    ENGS = ("pe", "act", "dve", "pool", "sp")
    BLK = {"pe": "tensor", "act": "scalar", "dve": "vector", "pool": "gpsimd", "sp": "sync"}

    def __init__(self, nc, n_lanes=16, epoch=30000):
        self.nc = nc
        self.epoch = epoch
        self.n_lanes = n_lanes
        self.lane_sem = [nc.alloc_semaphore("lane%d" % i) for i in range(n_lanes)]
        self.lane_val = [0] * n_lanes
        self.next_lane = 0
        self.cur_sem = {e: nc.alloc_semaphore("eng_%s_0" % e) for e in self.ENGS}
        self.cur_val = {e: 0 for e in self.ENGS}
        self.nsem = {e: 1 for e in self.ENGS}
        self.waited = {e: {} for e in self.ENGS}
        self.ops = []
        self.last_writer = {}
        self.readers = {}
        self.n_emitted = 0

    def capture(self, f):
        self._cap = []
        f()
        lst = self._cap
        self._cap = None
        return lst

    def emit_merged(self, A, B):
        na, nb = len(A), len(B)
        ia = ib = 0
        while ia < na or ib < nb:
            if ib >= nb or (ia < na and ia * nb <= ib * na):
                self.op(*A[ia])
                ia += 1
            else:
                self.op(*B[ib])
                ib += 1

    DEF_COST = {"pe": 0.3, "act": 0.45, "dve": 0.4, "pool": 0.5, "sp": 0.05}

    def op(self, eng, fn, reads=(), writes=(), dma=False, cost=None):
        if getattr(self, "_cap", None) is not None:
            self._cap.append((eng, fn, tuple(reads), tuple(writes), dma, cost))
            return -1
        i = len(self.ops)
        deps = set()
        for k in reads:
            w = self.last_writer.get(k)
            if w is not None:
                deps.add(w)
        for k in writes:
            w = self.last_writer.get(k)
            if w is not None:
                deps.add(w)
            for r in self.readers.get(k, ()):
                deps.add(r)
        for k in reads:
            self.readers.setdefault(k, []).append(i)
        for k in writes:
            self.last_writer[k] = i
            self.readers[k] = []
        o = _Op()
        o.eng = eng
        o.fn = fn
        o.deps = deps
        o.is_dma = dma
        o.sem = None
        o.val = 0
        o.lane_prev = 0
        o.cost = cost if cost is not None else (2.5 if dma else self.DEF_COST[eng])
        o.cost_given = False
        self.ops.append(o)
        return i

    def dma(self, out, in_, reads=(), writes=(), eng="sp", cost=None):
        return self.op(eng, lambda e: e.dma_start(out=out, in_=in_), reads, writes, dma=True, cost=cost)

    class _Fake:
        class _Ins:
            def then_inc(self, *a, **k):
                return self

        TBL = None

        def __init__(self, eng):
            self.eng = eng
            self.t = 0.0
            self.tbl = None

        def __getattr__(self, name):
            def f(*args, **kw):
                if name == "activation":
                    fn_ = kw.get("func")
                    if fn_ in (AF.Silu, AF.Sigmoid):
                        self.tbl = "silu"
                    elif fn_ in (AF.Exp, AF.Ln):
                        self.tbl = "exp"
                    elif fn_ == AF.Sqrt:
                        self.tbl = "sqrt"
                try:
                    if name in ("matmul", "transpose"):
                        r = kw.get("rhs", kw.get("in_"))
                        n = r.free_size() if name == "matmul" else 128
                        mul = 4.0 if r.dtype == F32 else 1.0
                        self.t += 0.065 + mul * n / 2200.0
                    elif name == "dma_start":
                        self.t += 2.0 + kw["out"].nbytes() / 150e3
                    else:
                        o = kw.get("out", kw.get("ap", args[0] if args else None))
                        fs = o.free_size() if o is not None else 64
                        if self.eng == "act":
                            self.t += 0.22 + fs / 1300.0
                        elif self.eng == "dve":
                            self.t += 0.12 + fs / 900.0
                        else:
                            self.t += 0.2 + fs / 520.0
                except Exception:
                    self.t += 0.4
                return Sched._Fake._Ins()
            return f

    def _estimate(self, o):
        fk = Sched._Fake(o.eng)
        try:
            o.fn(fk)
        except Exception:
            return None
        self._tbl[id(o)] = fk.tbl
        return fk.t if fk.t > 0 else None

    def _reorder(self):
        self._tbl = {}
        for o in self.ops:
            if not getattr(o, "cost_given", False):
                est = self._estimate(o)
                if est is not None:
                    o.cost = est
        import heapq
        ops = self.ops
        n = len(ops)
        succ = [[] for _ in range(n)]
        indeg = [0] * n
        for i, o in enumerate(ops):
            indeg[i] = len(o.deps)
            for d in o.deps:
                succ[d].append(i)
        bl = [0.0] * n
        for i in range(n - 1, -1, -1):
            m_ = 0.0
            for s in succ[i]:
                if bl[s] > m_:
                    m_ = bl[s]
            bl[i] = ops[i].cost + m_ + (0.15 if succ[i] else 0.0)
        finish = [0.0] * n
        tbl_of = [self._tbl.get(id(o)) for o in ops]
        cur_tbl = [None]
        ready_l = {e: [] for e in self.ENGS}
        free = {e: 0.0 for e in self.ENGS}
        ready_t = [0.0] * n
        for i, o in enumerate(ops):
            if indeg[i] == 0:
                ready_l[o.eng].append(i)
        order = []
        while len(order) < n:
            best = None
            for e in self.ENGS:
                rl = ready_l[e]
                if not rl:
                    continue
                tmin = min(ready_t[i] for i in rl)
                t_e = max(free[e], tmin)
                cand = None
                cand_same = None
                for i in rl:
                    if ready_t[i] <= t_e + 1e-9:
                        if cand is None or (bl[i], -i) > (bl[cand], -cand):
                            cand = i
                        if e == "act":
                            tb_ = tbl_of[i]
                            if tb_ is None or tb_ == cur_tbl[0]:
                                if cand_same is None or (bl[i], -i) > (bl[cand_same], -cand_same):
                                    cand_same = i
                if e == "act" and cand_same is not None:
                    cand = cand_same
                if best is None or (t_e, cand) < (best[0], best[2]):
                    best = (t_e, e, cand)
            assert best is not None, "reorder: cyclic deps"
            st, e, i = best
            ready_l[e].remove(i)
            o = ops[i]
            if e == "act" and tbl_of[i] is not None and tbl_of[i] != cur_tbl[0]:
                st += 1.3
                cur_tbl[0] = tbl_of[i]
            if o.is_dma:
                free[e] = st + 0.08
                finish[i] = st + o.cost
            else:
                free[e] = st + o.cost
                finish[i] = st + o.cost
            order.append(i)
            for s in succ[i]:
                hop = 0.0 if ops[s].eng == e and not o.is_dma else 0.15
                ready_t[s] = max(ready_t[s], finish[i] + hop)
                indeg[s] -= 1
                if indeg[s] == 0:
                    ready_l[ops[s].eng].append(s)
        remap = {old: new for new, old in enumerate(order)}
        new_ops = [ops[i] for i in order]
        for o in new_ops:
            o.deps = {remap[d] for d in o.deps}
        for k, o in enumerate(new_ops):
            for d in o.deps:
                assert d < k, "reorder produced non-topological order"
        self.ops = new_ops
        self.sim_span = max(finish) if n else 0.0

    def _wait(self, engobj, e, sem, val):
        w = self.waited[e]
        if w.get(sem.num, 0) >= val:
            return
        w[sem.num] = val
        engobj.wait_ge(sem, val)

    def _check(self, ops, per):
        semv = {}
        ptr = {e: 0 for e in self.ENGS}
        base = {}
        for o in ops:
            if o.sem is not None and o.sem.num not in base:
                base[o.sem.num] = o.val - (16 if o.is_dma else 1)
        semv.update(base)
        progress = True
        done = 0
        while progress:
            progress = False
            for e in self.ENGS:
                while ptr[e] < len(per[e]):
                    o = ops[per[e][ptr[e]]]
                    ok = True
                    for d in o.deps:
                        p = ops[d]
                        if p.sem is None:
                            continue
                        if p.eng == "pe" and e == "pe" and not p.is_dma and not o.is_dma:
                            continue
                        if semv.get(p.sem.num, 0) < p.val:
                            ok = False
                            break
                    if ok and o.is_dma and o.lane_prev > 0 and semv.get(o.sem.num, 0) < o.lane_prev:
                        ok = False
                    if not ok:
                        break
                    if o.sem is not None:
                        semv[o.sem.num] = semv.get(o.sem.num, 0) + (16 if o.is_dma else 1)
                        assert semv[o.sem.num] <= o.val + 16 * self.n_lanes, "sem overshoot"
                    ptr[e] += 1
                    done += 1
                    progress = True
        if done != len(ops):
            stuck = {e: per[e][ptr[e]] for e in self.ENGS if ptr[e] < len(per[e])}
            raise RuntimeError("SCHED deadlock: %d/%d ops done, stuck at %s" % (done, len(ops), stuck))
        print("SCHED_CHECK ok", len(ops), flush=True)

    def flush(self, reorder=True):
        if reorder and os.environ.get("NO_REORDER") != "1":
            self._reorder()
            if os.environ.get("SCHED_CHECK") == "1":
                print("sim_span_us", round(self.sim_span, 1), flush=True)
        ops = self.ops
        n = len(ops)
        nc = self.nc
        consumed = [False] * n
        for o in ops:
            for d in o.deps:
                p = ops[d]
                if p.eng == "pe" and o.eng == "pe" and not p.is_dma and not o.is_dma:
                    continue
                consumed[d] = True
        last_of = {}
        for i, o in enumerate(ops):
            if not o.is_dma:
                last_of[o.eng] = i
        for i, o in enumerate(ops):
            if o.is_dma:
                lane = self.next_lane
                self.next_lane = (lane + 1) % self.n_lanes
                o.lane_prev = self.lane_val[lane]
                self.lane_val[lane] += 16
                o.sem = self.lane_sem[lane]
                o.val = self.lane_val[lane]
            elif consumed[i] or last_of[o.eng] == i:
                e = o.eng
                if self.cur_val[e] >= self.epoch:
                    self.cur_sem[e] = nc.alloc_semaphore("eng_%s_%d" % (e, self.nsem[e]))
                    self.nsem[e] += 1
                    self.cur_val[e] = 0
                self.cur_val[e] += 1
                o.sem = self.cur_sem[e]
                o.val = self.cur_val[e]
        per = {e: [] for e in self.ENGS}
        for i, o in enumerate(ops):
            per[o.eng].append(i)
        if os.environ.get("SCHED_CHECK") == "1":
            self._check(ops, per)
        final_eng = {e: (self.cur_sem[e], self.cur_val[e]) for e in self.ENGS}
        final_lane = list(self.lane_val)
        with nc.Block() as block:
            for e in self.ENGS:
                def body(engobj, e=e):
                    for i in per[e]:
                        o = ops[i]
                        for d in sorted(o.deps):
                            p = ops[d]
                            if p.sem is None:
                                continue
                            if p.eng == "pe" and e == "pe" and not p.is_dma and not o.is_dma:
                                continue
                            self._wait(engobj, e, p.sem, p.val)
                        if o.is_dma and o.lane_prev > 0:
                            self._wait(engobj, e, o.sem, o.lane_prev)
                        ins = o.fn(engobj)
                        if o.sem is not None:
                            ins.then_inc(o.sem, 16 if o.is_dma else 1)
                    for e2 in self.ENGS:
                        if e2 != e and final_eng[e2][1] > 0:
                            self._wait(engobj, e, final_eng[e2][0], final_eng[e2][1])
                    for l in range(self.n_lanes):
                        if final_lane[l] > 0:
                            self._wait(engobj, e, self.lane_sem[l], final_lane[l])
                getattr(block, self.BLK[e])(body)
        self.n_emitted += n
        self.ops = []
        self.last_writer = {}
        self.readers = {}


class _Alloc:
    def __init__(self, nc):
        self.nc = nc
        self.es = ExitStack()

    def __enter__(self):
        self.es.__enter__()
        return self

    def __exit__(self, *a):
        return self.es.__exit__(*a)

    _uid = [0]

    def _nm(self, name):
        self._uid[0] += 1
        return "%s_s%d" % (name, self._uid[0])

    def sb(self, name, shape, dt):
        return self.es.enter_context(self.nc.sbuf_tensor(self._nm(name), shape, dt))

    def ps(self, name, shape, dt):
        return self.es.enter_context(self.nc.psum_tensor(self._nm(name), shape, dt))


def bc(ap, shape):
    return ap.broadcast_to(list(shape))


class _Stop(Exception):
    pass


def build(seq_lens, dump=(), upto=None):
    try:
        return _build(seq_lens, dump, upto)
    except _Stop as s:
        return s.args[0]


def _build(seq_lens, dump=(), upto=None):
    nc = bass.Bass("TRN2", target_bir_lowering=False)
    nc.allow_low_precision("bf16 matmul operands, fp32 accumulation")
    Stot = sum(seq_lens)
    Smax = max(seq_lens)
    uniqS = sorted(set(seq_lens))

    def din(name, shape, dt=F32):
        return nc.dram_tensor(name, list(shape), dt, kind="ExternalInput").ap()

    x_all = din("x_all", [Stot, D])
    w_in_l = din("w_in_l", [128, 8, NIN])
    w_so_l = din("w_so_l", [128, 16, D])
    w_fo_l = din("w_fo_l", [128, 8, D])
    w_o_l = din("w_o_l", [128, 8, D])
    w_gu_l = din("w_gu_l", [128, 8, 2 * DFF])
    w_d_l = din("w_d_l", [128, 22, D])
    convw_l = din("convw_l", [128, 32, 7])
    convb_l = din("convb_l", [128, 32])
    dtb_col = din("dtb_col", [128, 1])
    alog_col = din("alog_col", [128, 1])
    dskip_bc = din("dskip_bc", [128, 32])
    ssdn_bc = din("ssdn_bc", [128, DI])
    gmix_col = din("gmix_col", [128, 8])
    gffn_col = din("gffn_col", [128, 8])
    gfin_bc = din("gfin_bc", [128, D])
    bfo_col = din("bfo_col", [128, 8])
    ident_b_d = din("ident_b", [128, 128], BF16)
    ident_f_d = din("ident_f", [128, 128])
    antiid_d = din("antiid_b", [128, 128], BF16)
    tri_f_d = din("tri_f", [128, 5, 128])
    tri_b_d = din("tri_b", [128, 4, 128], BF16)
    neg_b_d = din("neg_b", [128, 2, 512], BF16)
    cs128_d = {S: din("cs128_%d" % S, [128, 384], BF16) for S in uniqS}
    tab_d = {S: din("tab_%d" % S, [S // 128, 128, S // 128, 2, 128], BF16) for S in uniqS}

    y_all = nc.dram_tensor("y_all", [Stot, D], F32, kind="ExternalOutput").ap()

    def dscr(name, shape, dt):
        kind = "ExternalOutput" if name in dump else "Internal"
        return nc.dram_tensor(name, list(shape), dt, kind=kind).ap()

    gatesT = dscr("gatesT", [2048, Smax], BF16)
    Utm = dscr("Utm", [Smax, D], BF16)
    ynT = dscr("ynT", [DI, Smax], BF16)
    YT = dscr("YT", [D, Smax], BF16)
    x2d = dscr("x2d", [Smax, D], F32)

    S_ = Sched(nc)
    op = S_.op
    dma = S_.dma

    with _Alloc(nc) as A1:
        ident_b = A1.sb("ident_b", [128, 128], BF16)
        ident_f = A1.sb("ident_f", [128, 128], F32)
        tri_f = A1.sb("tri_f", [128, 5, 128], F32)
        tri_b = A1.sb("tri_b", [128, 4, 128], BF16)
        neg_b = A1.sb("neg_b", [128, 2, 512], BF16)
        gmix = A1.sb("gmix", [128, 8], F32)
        gffn = A1.sb("gffn", [128, 8], F32)
        bfo = A1.sb("bfo", [128, 8], F32)
        dtb = A1.sb("dtb", [128, 1], F32)
        acol = A1.sb("acol", [128, 1], F32)
        dsk = A1.sb("dsk", [128, 32], F32)
        dma(ident_b[:], ident_b_d, writes=["c_idb"])
        dma(ident_f[:], ident_f_d, writes=["c_idf"])
        dma(tri_f[:], tri_f_d, writes=["c_trif"])
        dma(tri_b[:], tri_b_d, writes=["c_trib"])
        dma(neg_b[:], neg_b_d, writes=["c_negb"])
        dma(gmix[:], gmix_col, writes=["c_gmix"])
        dma(gffn[:], gffn_col, writes=["c_gffn"])
        dma(bfo[:], bfo_col, writes=["c_bfo"])
        dma(dtb[:], dtb_col, writes=["c_dtb"])
        dma(acol[:], alog_col, writes=["c_acol"])
        dma(dsk[:], dskip_bc, writes=["c_dsk"])
        op("act", lambda e: e.activation(out=acol[:], in_=acol[:], func=AF.Exp), reads=["c_acol"], writes=["c_acol"])
        op("dve", lambda e: e.tensor_scalar(out=acol[:], in0=acol[:], scalar1=-1.0, scalar2=None, op0=ALU.mult),
           reads=["c_acol"], writes=["c_acol"])
        S_.flush()
        if upto == 'const':
            raise _Stop(nc)

        def rms_to_T(tag, b, xt_ap, xn, ss, sd, rstd, pT_ap, gcol, dst_ap, rkeys, dst_key):
            op("act", lambda e: e.activation(out=xn[:], in_=xt_ap, func=AF.Square, accum_out=ss[:]),
               reads=rkeys, writes=[(tag, "xn", b), (tag, "ss", b)])
            op("act", lambda e: e.activation(out=sd[:], in_=ss[:], func=AF.Sqrt, scale=1.0 / D, bias=EPS),
               reads=[(tag, "ss", b)], writes=[(tag, "sd", b)])
            op("dve", lambda e: e.reciprocal(out=rstd[:], in_=sd[:]), reads=[(tag, "sd", b)], writes=[(tag, "rstd", b)])
            op("act", lambda e: e.activation(out=xn[:], in_=xt_ap, func=AF.Copy, scale=rstd[:]),
               reads=rkeys + [(tag, "rstd", b)], writes=[(tag, "xn", b)])

            def tr(e):
                for c in range(8):
                    ins = e.transpose(out=pT_ap[:, c, :], in_=xn[:, c * 128:(c + 1) * 128], identity=ident_b[:])
                return ins
            op("pe", tr, reads=[(tag, "xn", b)], writes=[(tag, "pT", b)])
            op("dve", lambda e: e.tensor_tensor(out=dst_ap, in0=pT_ap, in1=bc(gcol[:].unsqueeze(2), [128, 8, 128]),
                                                 op=ALU.mult),
               reads=[(tag, "pT", b)], writes=[dst_key])

        s0 = 0
        for si, S in enumerate(seq_lens):
            NT = S // 128
            NB = S // 512
            x_seq = x_all[s0:s0 + S, :]
            y_seq = y_all[s0:s0 + S, :]
            with _Alloc(nc) as A2:
                hT = A2.sb("hT", [128, 8, S], BF16)
                with _Alloc(nc) as A3:
                    xt = [A3.sb("xt%d" % k_, [128, D], F32) for k_ in range(4)]
                    xn = [A3.sb("xn%d" % k_, [128, D], BF16) for k_ in range(4)]
                    ss = [A3.sb("ss%d" % k_, [128, 1], F32) for k_ in range(4)]
                    sd = [A3.sb("sd%d" % k_, [128, 1], F32) for k_ in range(4)]
                    rs = [A3.sb("rs%d" % k_, [128, 1], F32) for k_ in range(4)]
                    pT1 = A3.ps("pT1", [128, 4, 8, 128], BF16)
                    for t in range(NT):
                        b = t % 4
                        dma(xt[b][:], x_seq[t * 128:(t + 1) * 128, :], writes=[("p1", "xt", b)])
                        rms_to_T("p1", b, xt[b][:], xn[b], ss[b], sd[b], rs[b], pT1[:, b], gmix,
                                 hT[:, :, t * 128:(t + 1) * 128], [("p1", "xt", b)], ("hT", t))
                    S_.flush()
                    if upto == 'p1':
                        raise _Stop(nc)

                with _Alloc(nc) as A4:
                    dt_tm = A4.sb("dt_tm", [128, NT, 128], F32)
                    with _Alloc(nc) as A5:
                        wt0 = A5.sb("wt0", [128, 8, 128], BF16)
                        wt1 = A5.sb("wt1", [128, 8, 128], BF16)
                        grow0 = A5.sb("grow0", [128, S], BF16)
                        grow1 = A5.sb("grow1", [128, S], BF16)
                        wu = A5.sb("wu", [128, 8, D], BF16)
                        urow0 = A5.sb("urow0", [128, D], BF16)
                        urow1 = A5.sb("urow1", [128, D], BF16)
                        wdt = A5.sb("wdt", [128, 8, 128], BF16)
                        dtx = A5.sb("dtx", [128, 512], F32)
                        dta = A5.sb("dta", [128, 512], F32)
                        dte = A5.sb("dte", [128, 512], F32)
                        dts = A5.sb("dts", [128, 512], F32)
                        psA = A5.ps("psA", [128, 4, 512], F32)
                        psU = A5.ps("psU", [128, 2, 2, 512], F32)
                        wt = [wt0, wt1]; grow = [grow0, grow1]; urow = [urow0, urow1]
                        nmm = 0
                        for j in range(16):
                            b = j % 2
                            c0 = OFF_G + j * 128
                            dma(wt[b][:], w_in_l[:, :, c0:c0 + 128], writes=[("wt", b)], eng="pool")
                            for blk in range(NB):
                                bank = nmm % 4
                                nmm += 1

                                def mm(e, b=b, blk=blk, bank=bank):
                                    for c in range(8):
                                        ins = e.matmul(psA[:, bank, :], lhsT=wt[b][:, c, :],
                                                       rhs=hT[:, c, blk * 512:(blk + 1) * 512], start=(c == 0), stop=(c == 7))
                                    return ins
                                op("pe", mm, reads=[("wt", b)] + [("hT", t) for t in range(blk * 4, blk * 4 + 4)],
                                   writes=[("psA", bank)])
                                op("act", lambda e, b=b, blk=blk, bank=bank: e.activation(
                                    out=grow[b][:, blk * 512:(blk + 1) * 512], in_=psA[:, bank, :], func=AF.Sigmoid),
                                   reads=[("psA", bank)], writes=[("grow", b, blk)])
                            dma(gatesT[j * 128:(j + 1) * 128, 0:S], grow[b][:],
                                reads=[("grow", b, blk) for blk in range(NB)], writes=[("gatesT", j)])
                        dma(wu[:], w_in_l[:, :, OFF_U:OFF_U + D], writes=["wu"], eng="pool")
                        for t in range(NT):
                            b = t % 2

                            def mmu(e, t=t, b=b):
                                for hf in range(2):
                                    for c in range(8):
                                        ins = e.matmul(psU[:, b, hf, :], lhsT=hT[:, c, t * 128:(t + 1) * 128],
                                                       rhs=wu[:, c, hf * 512:(hf + 1) * 512], start=(c == 0), stop=(c == 7))
                                return ins
                            op("pe", mmu, reads=["wu", ("hT", t)], writes=[("psU", b)])
                            op("dve", lambda e, b=b: e.tensor_copy(out=urow[b][:], in_=psU[:, b].rearrange("p a b -> p (a b)")),
                               reads=[("psU", b)], writes=[("urow", b)])
                            dma(Utm[t * 128:(t + 1) * 128, :], urow[b][:], reads=[("urow", b)], writes=[("Utm", t)])
                        dma(wdt[:, :, 0:64], w_in_l[:, :, OFF_DT:OFF_DT + 64], writes=["wdt"], eng="pool")
                        dma(wdt[:, :, 64:128], w_in_l[:, :, OFF_DT:OFF_DT + 64], writes=["wdt2"], eng="pool")
                        for blk in range(NB):
                            bank = nmm % 4
                            nmm += 1

                            def mmd(e, blk=blk, bank=bank):
                                for c in range(8):
                                    ins = e.matmul(psA[:, bank, :], lhsT=wdt[:, c, :],
                                                   rhs=hT[:, c, blk * 512:(blk + 1) * 512], start=(c == 0), stop=(c == 7))
                                return ins
                            op("pe", mmd, reads=["wdt", "wdt2"] + [("hT", t) for t in range(blk * 4, blk * 4 + 4)],
                               writes=[("psA", bank)])
                            op("act", lambda e, bank=bank: e.activation(out=dtx[:], in_=psA[:, bank, :], func=AF.Identity,
                                                                         bias=dtb[:]),
                               reads=[("psA", bank)], writes=["dtx"])
                            op("act", lambda e: e.activation(out=dta[:], in_=dtx[:], func=AF.Abs),
                               reads=["dtx"], writes=["dta"])
                            op("act", lambda e: e.activation(out=dte[:], in_=dta[:], func=AF.Exp, scale=-1.0),
                               reads=["dta"], writes=["dte"])
                            op("act", lambda e: e.activation(out=dte[:], in_=dte[:], func=AF.Ln, bias=1.0),
                               reads=["dte"], writes=["dte"])
                            op("dve", lambda e: e.scalar_tensor_tensor(out=dts[:], in0=dtx[:], scalar=0.0, in1=dte[:],
                                                                       op0=ALU.max, op1=ALU.add),
                               reads=["dtx", "dte"], writes=["dts"])
                            op("dve", lambda e: e.tensor_scalar(out=dts[64:128, :], in0=dts[64:128, :],
                                                                scalar1=acol[64:128, :], scalar2=None, op0=ALU.mult),
                               reads=["dts"], writes=["dts"])

                            def trd(e, blk=blk):
                                for q in range(4):
                                    ins = e.transpose(out=psU[:, 0, 0, q * 128:(q + 1) * 128],
                                                      in_=dts[:, q * 128:(q + 1) * 128], identity=ident_f[:])
                                return ins
                            op("pe", trd, reads=["dts"], writes=[("psU", 0)])
                            op("dve", lambda e, blk=blk: e.tensor_copy(
                                out=dt_tm[:, blk * 4:(blk + 1) * 4, :].rearrange("p a b -> p (a b)"), in_=psU[:, 0, 0, :]),
                               reads=[("psU", 0)], writes=[("dt_tm", blk)])
                        S_.flush()
                        if upto == 'p2a':
                            raise _Stop(nc)

                    with _Alloc(nc) as A6:
                        wta = A6.sb("wta", [128, 8, 128], BF16)
                        wtb = A6.sb("wtb", [128, 8, 128], BF16)
                        wz = A6.sb("wz", [128, 8, 256], BF16)
                        ppad = A6.sb("ppad", [128, S + 8], BF16)
                        szall = A6.sb("szall", [128, NT, 256], BF16)
                        xc0 = A6.sb("xc0", [128, S], BF16)
                        xc1 = A6.sb("xc1", [128, S], BF16)
                        xcB = A6.sb("xcB", [128, S], BF16)
                        xcC = A6.sb("xcC", [128, S], BF16)
                        hbsn = A6.sb("hbsn", [128, NT, 256], BF16)
                        ynTg0 = A6.sb("ynTg0", [128, 2, 512], BF16)
                        ynTg1 = A6.sb("ynTg1", [128, 2, 512], BF16)
                        ynTgs = [ynTg0, ynTg1]
                        ssdn = A6.sb("ssdn", [128, 256], F32)
                        cw = A6.sb("cw", [128, 32, 7], F32)
                        cb = A6.sb("cb", [128, 32], F32)
                        dg = A6.sb("dg", [128, 7, 128], BF16)
                        xb0 = A6.sb("xb0", [128, 384], BF16)
                        xb1 = A6.sb("xb1", [128, 384], BF16)
                        cps = A6.sb("cps", [128, NT, 16], F32)
                        evarg = A6.sb("evarg", [128, NT, 24], F32)
                        evall = A6.sb("evall", [128, NT, 24], F32)
                        cfall = A6.sb("cfall", [128, NT, 8], F32)
                        xws = [A6.sb("xw%d" % k_, [128, 256], BF16) for k_ in range(2)]
                        xdfs = [A6.sb("xdf%d" % k_, [128, 256], BF16) for k_ in range(2)]
                        xdbs = [A6.sb("xdb%d" % k_, [128, 256], BF16) for k_ in range(2)]
                        xDs = [A6.sb("xD%d" % k_, [128, 256], BF16) for k_ in range(2)]
                        rDf = A6.sb("rDf", [128, 4, 128], BF16)
                        rDb = A6.sb("rDb", [128, 4, 128], BF16)
                        Df = A6.sb("Df", [128, 4, 128], BF16)
                        Db = A6.sb("Db", [128, 4, 128], BF16)
                        Mf = A6.sb("Mf", [128, 4, 128], BF16)
                        Mb = A6.sb("Mb", [128, 4, 128], BF16)
                        t1s = [A6.sb("t1%d" % k_, [128, 256], F32) for k_ in range(2)]
                        t2s = [A6.sb("t2%d" % k_, [128, 256], F32) for k_ in range(2)]
                        yzs = [A6.sb("yz%d" % k_, [128, 256], F32) for k_ in range(2)]
                        sqj = A6.sb("sqj", [128, 256], BF16)
                        ssq = A6.sb("ssq", [128, 1], F32)
                        lnv = A6.sb("lnv", [128, 1], F32)
                        rsv = A6.sb("rsv", [128, 1], F32)
                        yn = A6.sb("yn", [128, 256], BF16)
                        Hf = A6.sb("Hf", [128, 256], F32)
                        Hb = A6.sb("Hb", [128, 256], F32)
                        Hfbs = [A6.sb("Hfb%d" % k_, [128, 256], BF16) for k_ in range(2)]
                        qA = A6.ps("qA", [128, 2, 512], F32)
                        qO = A6.ps("qO", [128, 512], F32)
                        qI = A6.ps("qI", [128, 512], F32)
                        qC = A6.ps("qC", [128, 2, 512], F32)
                        qD = A6.ps("qD", [128, 2, 512], F32)
                        wtt = [wta, wtb]
                        xb = [xb0, xb1]
                        if os.environ.get("SCHED_CHECK") == "1":
                            print("P2b sbuf remaining", nc.sbuf_bytes_remaining, flush=True)
                        dma(cw[:], convw_l, writes=["cw"])
                        dma(cb[:], convb_l, writes=["cb"])
                        op("pool", lambda e: e.memset(ppad[:], 0.0), writes=["ppad_pad"])
                        cnt = {'mm': 0, 'w': 0}
                        def do_group(g):
                            nonlocal_cnt = cnt
                            tiles = [(OFF_X + (2 * g) * 128, xc0, "x0"), (OFF_X + (2 * g + 1) * 128, xc1, "x1"),
                                     (OFF_B + g * 128, xcB, "B"), (OFF_C + g * 128, xcC, "C")]
                            dma(wz[:], w_in_l[:, :, OFF_Z + g * 256:OFF_Z + (g + 1) * 256], writes=["wz"], eng="pool")
                            dma(ssdn[:], ssdn_bc[:, g * 256:(g + 1) * 256], writes=["ssdn"])
                            for (c0, xc, nm) in tiles:
                                ct = (c0 - OFF_X) // 128
                                wb = cnt['w'] % 2
                                cnt['w'] += 1
                                dma(wtt[wb][:], w_in_l[:, :, c0:c0 + 128], writes=[("wtt", wb)], eng="pool")
                                for blk in range(NB):
                                    bank = cnt['mm'] % 2
                                    cnt['mm'] += 1

                                    def mm(e, wb=wb, blk=blk, bank=bank):
                                        for c in range(8):
                                            ins = e.matmul(qI[:], lhsT=wtt[wb][:, c, :],
                                                           rhs=hT[:, c, blk * 512:(blk + 1) * 512], start=(c == 0), stop=(c == 7))
                                        return ins
                                    op("pe", mm, reads=[("wtt", wb)], writes=["bk_qI"])
                                    op("act", lambda e, blk=blk, bank=bank: e.activation(
                                        out=ppad[:, 3 + blk * 512:3 + (blk + 1) * 512], in_=qI[:], func=AF.Copy),
                                       reads=["ppad_pad"], writes=[("ppad", blk), "bk_qI"])
                                op("dve", lambda e, ct=ct: e.tensor_tensor(
                                    out=dg[:], in0=bc(ident_b[:].unsqueeze(1), [128, 7, 128]),
                                    in1=bc(cw[:, ct, :].unsqueeze(2), [128, 7, 128]), op=ALU.mult),
                                   reads=["cw"], writes=["dg"])
                                for blk in range(NB):
                                    bank = cnt['mm'] % 2
                                    cnt['mm'] += 1

                                    def mmc(e, blk=blk, bank=bank):
                                        for k in range(7):
                                            ins = e.matmul(qI[:], lhsT=dg[:, k, :],
                                                           rhs=ppad[:, blk * 512 + k:blk * 512 + k + 512],
                                                           start=(k == 0), stop=(k == 6))
                                        return ins
                                    rk = [("ppad", b2) for b2 in range(max(0, blk - 1), min(NB, blk + 2))]
                                    op("pe", mmc, reads=["dg", "ppad_pad"] + rk, writes=["bk_qI"])
                                    op("act", lambda e, blk=blk, bank=bank, xc=xc, ct=ct: e.activation(
                                        out=xc[:, blk * 512:(blk + 1) * 512], in_=qI[:], func=AF.Silu,
                                        bias=cb[:, ct:ct + 1]),
                                       reads=["cb"], writes=[("xc", nm, blk), "bk_qI"])
                            if float(os.environ.get('DBG_LVL', '9')) < 1:
                                return
                            gf = slice(4 * g, 4 * g + 4)
                            gb = slice(32 + 4 * g, 32 + 4 * g + 4)
                            af = slice(64 + 4 * g, 64 + 4 * g + 4)
                            ab = slice(96 + 4 * g, 96 + 4 * g + 4)

                            def x4(ap):
                                return ap.rearrange("p (h d) -> p h d", h=4)

                            def b4(ap):
                                return bc(ap.unsqueeze(2), [128, 4, 64])

                            def csall(e):
                                for c in range(NT):
                                    rhs_ = dt_tm[:, c, 64:128].rearrange("p (d h) -> p d h", d=2)[:, :, 4 * g:4 * g + 4]
                                    e.matmul(qO[:, c * 16:c * 16 + 8].rearrange("p (d h) -> p d h", d=2), lhsT=tri_f[:, 0, :],
                                             rhs=rhs_, start=True, stop=True)
                                    ins = e.matmul(qO[:, c * 16 + 8:c * 16 + 16].rearrange("p (d h) -> p d h", d=2),
                                                   lhsT=tri_f[:, 4, :], rhs=rhs_, start=True, stop=True)
                                return ins
                            op("pe", csall, reads=[("dt_tm", b_) for b_ in range(NB)], writes=["qO"], cost=0.8 * NT)
                            op("act", lambda e: e.activation(out=cps[:].rearrange("p c k -> p (c k)"), in_=qO[:, 0:NT * 16],
                                                             func=AF.Copy),
                               reads=["qO"], writes=["cps"])
                            op("dve", lambda e: e.tensor_copy(out=evarg[:, :, 0:4], in_=cps[:, :, 0:4]), reads=["cps"], writes=["ea0"])
                            op("dve", lambda e: e.tensor_tensor(out=evarg[:, :, 4:8], in0=cps[:, :, 8:12], in1=cps[:, :, 0:4],
                                                                op=ALU.subtract), reads=["cps"], writes=["ea1"])
                            op("dve", lambda e: e.tensor_copy(out=evarg[:, :, 8:12], in_=cps[:, :, 8:12]), reads=["cps"], writes=["ea2"])
                            op("dve", lambda e: e.tensor_tensor(out=evarg[:, :, 16:20], in0=cps[:, :, 4:8], in1=dt_tm[:, :, ab],
                                                                op=ALU.subtract),
                               reads=["cps"] + [("dt_tm", b_) for b_ in range(NB)], writes=["ea4"])
                            op("dve", lambda e: e.tensor_tensor(out=evarg[:, :, 12:16], in0=cps[:, :, 12:16], in1=evarg[:, :, 16:20],
                                                                op=ALU.subtract), reads=["cps", "ea4"], writes=["ea3"])
                            op("dve", lambda e: e.tensor_copy(out=evarg[:, :, 20:24], in_=cps[:, :, 12:16]), reads=["cps"], writes=["ea5"])
                            op("act", lambda e: e.activation(out=evall[:].rearrange("p c k -> p (c k)"),
                                                             in_=evarg[:].rearrange("p c k -> p (c k)"), func=AF.Exp),
                               reads=["ea0", "ea1", "ea2", "ea3", "ea4", "ea5"], writes=["evall"])
                            op("dve", lambda e: e.tensor_tensor(out=cfall[:, :, 0:4], in0=dt_tm[:, :, gf], in1=evall[:, :, 4:8],
                                                                op=ALU.mult),
                               reads=["evall"] + [("dt_tm", b_) for b_ in range(NB)], writes=["cfall0"])
                            op("dve", lambda e: e.tensor_tensor(out=cfall[:, :, 4:8], in0=dt_tm[:, :, gb], in1=evall[:, :, 16:20],
                                                                op=ALU.mult),
                               reads=["evall"] + [("dt_tm", b_) for b_ in range(NB)], writes=["cfall1"])

                            for c in range(NT):
                                pb = c % 2

                                def mmz(e, c=c, pb=pb):
                                    for cc in range(8):
                                        ins = e.matmul(qC[:, pb, 0:256], lhsT=hT[:, cc, c * 128:(c + 1) * 128], rhs=wz[:, cc, :],
                                                       start=(cc == 0), stop=(cc == 7))
                                    return ins
                                op("pe", mmz, reads=["wz"], writes=[("bk_qC", pb)], cost=1.2)
                                op("act", lambda e, c=c, pb=pb: e.activation(out=szall[:, c, :], in_=qC[:, pb, 0:256], func=AF.Silu),
                                   reads=[], writes=[("szall", c), ("bk_qC", pb)])

                            def do_transposes(c, pb):
                                cs = slice(c * 128, (c + 1) * 128)
                                blk = c // 4

                                def tr(e):
                                    e.matmul(qD[:, pb, 128:256], lhsT=xc0[:, cs], rhs=ident_b[:], start=True, stop=True)
                                    e.matmul(qD[:, pb, 256:384], lhsT=xc1[:, cs], rhs=ident_b[:], start=True, stop=True)
                                    return e.matmul(qD[:, pb, 384:512], lhsT=xcB[:, cs], rhs=ident_b[:], start=True, stop=True)
                                op("pe", tr, reads=[("xc", "x0", blk), ("xc", "x1", blk), ("xc", "B", blk)],
                                   writes=[("bk_qD", pb)], cost=0.4)
                                op("act", lambda e: e.activation(out=xb[pb][:], in_=qD[:, pb, 128:512], func=AF.Copy),
                                   reads=[], writes=[("xb", pb), ("bk_qD", pb)])

                            op("pool", lambda e: e.memset(Hb[:], 0.0), writes=["Hb"])
                            op("pool", lambda e: e.memset(Hf[:], 0.0), writes=["Hf"])
                            op("pool", lambda e: e.memset(Hfbs[0][:], 0.0), writes=[("Hfb", 0)])

                            for c in range(NT - 1, -1, -1):
                                pb = c % 2
                                xw = xws[pb]
                                do_transposes(c, pb)
                                op("pool", lambda e, c=c, pb=pb, xw=xw: e.tensor_tensor(
                                    out=x4(xw[:]), in0=x4(xb[pb][:, 0:256]), in1=b4(cfall[:, c, 4:8]), op=ALU.mult),
                                   reads=["cfall1", ("xb", pb)], writes=[("xw", pb)])
                                op("pe", lambda e, pb=pb, xw=xw: e.matmul(qC[:, pb, 0:256], lhsT=xb[pb][:, 256:384], rhs=xw[:],
                                                                          start=True, stop=True),
                                   reads=[("xw", pb), ("xb", pb)], writes=[("bk_qC", pb)])
                                op("act", lambda e, c=c: e.activation(out=hbsn[:, c, :], in_=Hb[:], func=AF.Copy),
                                   reads=["Hb"], writes=[("hbsn", c)])
                                op("dve", lambda e, c=c: e.tensor_tensor(out=x4(Hb[:]), in0=x4(Hb[:]), in1=b4(evall[:, c, 20:24]),
                                                                         op=ALU.mult),
                                   reads=["Hb", "evall"], writes=["Hb"])
                                op("dve", lambda e, pb=pb: e.tensor_tensor(out=Hb[:], in0=Hb[:], in1=qC[:, pb, 0:256], op=ALU.add),
                                   reads=["Hb"], writes=["Hb", ("bk_qC", pb)])

                            for c in range(NT):
                                pb = c % 2
                                cs = slice(c * 128, (c + 1) * 128)
                                blk = c // 4
                                xw, xdf, xdb, xD = xws[pb], xdfs[pb], xdbs[pb], xDs[pb]
                                t1, t2, yz = t1s[pb], t2s[pb], yzs[pb]
                                ynb_ = ynTgs[blk % 2]
                                do_transposes(c, pb)
                                op("pe", lambda e, cs=cs, pb=pb: e.matmul(qD[:, pb, 0:128], lhsT=xcB[:, cs], rhs=xcC[:, cs],
                                                                          start=True, stop=True),
                                   reads=[("xc", "B", blk), ("xc", "C", blk)], writes=[("bk_qD", pb)])
                                op("pool", lambda e, c=c: e.tensor_tensor(
                                    out=rDf[:], in0=bc(dt_tm[:, c, af].unsqueeze(2), [128, 4, 128]),
                                    in1=bc(tri_b[:, 0:1, :], [128, 4, 128]), op=ALU.mult),
                                   reads=[("dt_tm", blk)], writes=["rDf"], cost=0.8)
                                op("pool", lambda e, c=c: e.tensor_tensor(
                                    out=rDb[:], in0=bc(dt_tm[:, c, ab].unsqueeze(2), [128, 4, 128]),
                                    in1=bc(tri_b[:, 2:3, :], [128, 4, 128]), op=ALU.mult),
                                   reads=[("dt_tm", blk)], writes=["rDb"], cost=0.8)

                                def segf(e):
                                    e.matmul(qA[:, 0, :], lhsT=tri_b[:, 1, :], rhs=rDf[:].rearrange("p h l -> p (h l)"),
                                             start=True, stop=False)
                                    return e.matmul(qA[:, 0, :], lhsT=ident_b[:], rhs=neg_b[:, 0, :], start=False, stop=True)

                                def segb(e):
                                    e.matmul(qA[:, 1, :], lhsT=tri_b[:, 3, :], rhs=rDb[:].rearrange("p h l -> p (h l)"),
                                             start=True, stop=False)
                                    return e.matmul(qA[:, 1, :], lhsT=ident_b[:], rhs=neg_b[:, 1, :], start=False, stop=True)
                                op("pe", segf, reads=["rDf"], writes=[("qA", 0)], cost=0.6)
                                op("pe", segb, reads=["rDb"], writes=[("qA", 1)], cost=0.6)
                                op("act", lambda e: e.activation(out=Df[:].rearrange("p h l -> p (h l)"), in_=qA[:, 0, :], func=AF.Exp),
                                   reads=[], writes=["Df", ("qA", 0)], cost=0.6)
                                op("act", lambda e: e.activation(out=Db[:].rearrange("p h l -> p (h l)"), in_=qA[:, 1, :], func=AF.Exp),
                                   reads=[], writes=["Db", ("qA", 1)], cost=0.6)
                                op("dve", lambda e, pb=pb: e.tensor_tensor(
                                    out=Mf[:], in0=Df[:], in1=bc(qD[:, pb, 0:128].unsqueeze(1), [128, 4, 128]), op=ALU.mult),
                                   reads=["Df"], writes=["Mf", ("bk_qD", pb)], cost=0.7)
                                op("dve", lambda e, pb=pb: e.tensor_tensor(
                                    out=Mb[:], in0=Db[:], in1=bc(qD[:, pb, 0:128].unsqueeze(1), [128, 4, 128]), op=ALU.mult),
                                   reads=["Db"], writes=["Mb", ("bk_qD", pb)], cost=0.7)
                                op("dve", lambda e, c=c, pb=pb, xdf=xdf: e.tensor_tensor(
                                    out=x4(xdf[:]), in0=x4(xb[pb][:, 0:256]), in1=b4(dt_tm[:, c, gf]), op=ALU.mult),
                                   reads=[("xb", pb), ("dt_tm", blk)], writes=[("xdf", pb)])
                                op("dve", lambda e, c=c, pb=pb, xdb=xdb: e.tensor_tensor(
                                    out=x4(xdb[:]), in0=x4(xb[pb][:, 0:256]), in1=b4(dt_tm[:, c, gb]), op=ALU.mult),
                                   reads=[("xb", pb), ("dt_tm", blk)], writes=[("xdb", pb)])
                                op("pool", lambda e, pb=pb, xD=xD: e.tensor_tensor(
                                    out=x4(xD[:]), in0=x4(xb[pb][:, 0:256]), in1=b4(dsk[:, gf]), op=ALU.mult),
                                   reads=[("xb", pb)], writes=[("xD", pb)])
                                op("pool", lambda e, c=c, pb=pb, xw=xw: e.tensor_tensor(
                                    out=x4(xw[:]), in0=x4(xb[pb][:, 0:256]), in1=b4(cfall[:, c, 0:4]), op=ALU.mult),
                                   reads=[("xb", pb), "cfall0"], writes=[("xw", pb)])

                                def ydiag(e, pb=pb, xdf=xdf, xdb=xdb, xD=xD):
                                    for h in range(4):
                                        hs = slice(h * 64, (h + 1) * 64)
                                        o_ = qC[:, pb, 256 + h * 64:256 + (h + 1) * 64]
                                        e.matmul(o_, lhsT=Mf[:, h, :], rhs=xdf[:, hs], start=True, stop=False)
                                        e.matmul(o_, lhsT=Mb[:, h, :], rhs=xdb[:, hs], start=False, stop=False)
                                        ins = e.matmul(o_, lhsT=ident_b[:], rhs=xD[:, hs], start=False, stop=True)
                                    return ins
                                op("pe", ydiag, reads=["Mf", "Mb", ("xdf", pb), ("xdb", pb), ("xD", pb)], writes=[("bk_qC", pb)], cost=1.0)
                                op("pe", lambda e, pb=pb, xw=xw: e.matmul(qC[:, pb, 0:256], lhsT=xb[pb][:, 256:384], rhs=xw[:],
                                                                          start=True, stop=True),
                                   reads=[("xw", pb), ("xb", pb)], writes=[("bk_qC", pb)])

                                def yoff(e, c=c, cs=cs):
                                    e.matmul(qO[:, 0:256], lhsT=xcC[:, cs], rhs=Hfbs[c % 2][:], start=True, stop=True)
                                    return e.matmul(qO[:, 256:512], lhsT=xcC[:, cs], rhs=hbsn[:, c, :], start=True, stop=True)
                                op("pe", yoff, reads=[("xc", "C", blk), ("Hfb", c % 2), ("hbsn", c)], writes=["qO"], cost=0.45)
                                op("dve", lambda e, c=c: e.tensor_tensor(out=x4(Hf[:]), in0=x4(Hf[:]), in1=b4(evall[:, c, 8:12]),
                                                                         op=ALU.mult),
                                   reads=["Hf", "evall"], writes=["Hf"])
                                op("dve", lambda e, pb=pb: e.tensor_tensor(out=Hf[:], in0=Hf[:], in1=qC[:, pb, 0:256], op=ALU.add),
                                   reads=["Hf"], writes=["Hf", ("bk_qC", pb)])
                                op("act", lambda e, c=c: e.activation(out=Hfbs[(c + 1) % 2][:], in_=Hf[:], func=AF.Copy),
                                   reads=["Hf"], writes=[("Hfb", (c + 1) % 2)])
                                op("dve", lambda e, c=c, t1=t1: e.tensor_tensor(out=x4(t1[:]), in0=x4(qO[:, 0:256]),
                                                                                in1=b4(evall[:, c, 0:4]), op=ALU.mult),
                                   reads=["evall"], writes=[("t1", pb), "qO"])
                                op("dve", lambda e, c=c, t2=t2: e.tensor_tensor(out=x4(t2[:]), in0=x4(qO[:, 256:512]),
                                                                                in1=b4(evall[:, c, 12:16]), op=ALU.mult),
                                   reads=["evall"], writes=[("t2", pb), "qO"])
                                op("pool", lambda e, t1=t1, t2=t2: e.tensor_tensor(out=t1[:], in0=t1[:], in1=t2[:], op=ALU.add),
                                   reads=[("t2", pb)], writes=[("t1", pb)])
                                op("dve", lambda e, pb=pb, t1=t1: e.tensor_tensor(out=t1[:], in0=t1[:], in1=qC[:, pb, 256:512],
                                                                                  op=ALU.add),
                                   reads=[], writes=[("t1", pb), ("bk_qC", pb)])
                                op("dve", lambda e, c=c, t1=t1, yz=yz: e.tensor_tensor(out=yz[:], in0=t1[:], in1=szall[:, c, :],
                                                                                       op=ALU.mult),
                                   reads=[("t1", pb), ("szall", c)], writes=[("yz", pb)])
                                op("act", lambda e, yz=yz: e.activation(out=sqj[:], in_=yz[:], func=AF.Square, accum_out=ssq[:]),
                                   reads=[("yz", pb)], writes=["sqj", "ssq"])
                                op("act", lambda e: e.activation(out=lnv[:], in_=ssq[:], func=AF.Ln, scale=1.0 / 256, bias=EPS),
                                   reads=["ssq"], writes=["lnv"], cost=0.25)
                                op("act", lambda e: e.activation(out=rsv[:], in_=lnv[:], func=AF.Exp, scale=-0.5),
                                   reads=["lnv"], writes=["rsv"], cost=0.25)
                                op("pool", lambda e, yz=yz, t2=t2: e.tensor_tensor(out=t2[:], in0=yz[:], in1=ssdn[:], op=ALU.mult),
                                   reads=[("yz", pb), "ssdn"], writes=[("t2", pb)])
                                op("act", lambda e, t2=t2: e.activation(out=yn[:], in_=t2[:], func=AF.Copy, scale=rsv[:]),
                                   reads=[("t2", pb), "rsv"], writes=["yn"])

                                def try_(e, pb=pb):
                                    e.matmul(qC[:, pb, 256:384], lhsT=yn[:, 0:128], rhs=ident_b[:], start=True, stop=True)
                                    return e.matmul(qC[:, pb, 384:512], lhsT=yn[:, 128:256], rhs=ident_b[:], start=True, stop=True)
                                op("pe", try_, reads=["yn"], writes=[("bk_qC", pb)])
                                op("act", lambda e, c=c, pb=pb, ynb_=ynb_: e.activation(
                                    out=ynb_[:, :, (c % 4) * 128:(c % 4 + 1) * 128],
                                    in_=qC[:, pb, 256:512].rearrange("p (a b) -> p a b", a=2), func=AF.Copy),
                                   reads=[], writes=[("ynTg", blk % 2, c % 4), ("bk_qC", pb)])
                                if c % 4 == 3:
                                    dma(ynT[g * 256:(g + 1) * 256, blk * 512:(blk + 1) * 512].rearrange("(k p) s -> p k s", p=128),
                                        ynb_[:], reads=[("ynTg", blk % 2, q_) for q_ in range(4)], writes=[("ynT", g, blk)])
                        for g_ in range(int(os.environ.get('DBG_NG', '8'))):
                            do_group(g_)
                        S_.flush()
                        if upto == 'p2b':
                            raise _Stop(nc)

            with _Alloc(nc) as A7:
                Usb = A7.sb("Usb", [128, NT, D], BF16)
                YTa = A7.sb("YTa", [128, 8, S], BF16)
                tb0 = A7.sb("tb0", [128, NT, 2, 128], BF16)
                tb1 = A7.sb("tb1", [128, NT, 2, 128], BF16)
                cs128 = A7.sb("cs128", [128, 384], BF16)
                antiid = A7.sb("antiid", [128, 128], BF16)
                vwTR = A7.sb("vwTR", [128, 16, 128], BF16)
                vw = A7.sb("vw", [128, 2, D], BF16)
                vwT = A7.sb("vwT", [128, 16, 128], BF16)
                fV = A7.ps("fV", [128, 2, 2, 512], F32)
                fT = A7.ps("fT", [128, 8, 128], F32)
                fY = A7.ps("fY", [128, 2, 512], F32)
                tb = [tb0, tb1]
                dma(cs128[:], cs128_d[S], writes=["cs128"])
                dma(antiid[:], antiid_d, writes=["antiid"])
                for t in range(NT):
                    dma(Usb[:, t, :], Utm[t * 128:(t + 1) * 128, :], writes=[("Usb", t)])
                for kt in range(NT // 2 + 1):
                    b = kt % 2
                    dma(tb[b][:], tab_d[S][kt], writes=[("tb", b)])

                    def mmf(e, b=b):
                        for cs_ in range(2):
                            for hf in range(2):
                                for t in range(NT):
                                    ins = e.matmul(fV[:, cs_, hf, :], lhsT=tb[b][:, t, cs_, :],
                                                   rhs=Usb[:, t, hf * 512:(hf + 1) * 512], start=(t == 0), stop=(t == NT - 1))
                        return ins
                    op("pe", mmf, reads=[("tb", b)] + [("Usb", t) for t in range(NT)], writes=["fV"])
                    op("act", lambda e: e.activation(out=vw[:, 0, :], in_=fV[:, 0].rearrange("p a b -> p (a b)"), func=AF.Copy),
                       reads=["fV"], writes=["vw0"])
                    op("dve", lambda e: e.tensor_copy(out=vw[:, 1, :], in_=fV[:, 1].rearrange("p a b -> p (a b)")),
                       reads=["fV"], writes=["vw1"])

                    def trf0(e):
                        for q in range(8):
                            ins = e.matmul(fT[:, q, :], lhsT=vw[:, 0, q * 128:(q + 1) * 128], rhs=ident_b[:],
                                           start=True, stop=True)
                        return ins

                    def trf1(e):
                        for q in range(8):
                            ins = e.matmul(fT[:, q, :], lhsT=vw[:, 1, q * 128:(q + 1) * 128], rhs=ident_b[:],
                                           start=True, stop=True)
                        return ins
                    op("pe", trf0, reads=["vw0"], writes=["fT"])
                    op("act", lambda e: e.activation(out=vwT[:, 0:8, :], in_=fT[:], func=AF.Copy),
                       reads=["fT"], writes=["vwT0"])
                    op("pe", trf1, reads=["vw1"], writes=["fT"])
                    op("dve", lambda e: e.tensor_copy(out=vwT[:, 8:16, :], in_=fT[:]),
                       reads=["fT"], writes=["vwT1"])

                    def mmy(e):
                        for g in range(8):
                            o_ = fY[:, g // 4, (g % 4) * 128:(g % 4 + 1) * 128]
                            e.matmul(o_, lhsT=cs128[:, 0:128], rhs=vwT[:, g, :], start=True, stop=False)
                            ins = e.matmul(o_, lhsT=cs128[:, 128:256], rhs=vwT[:, 8 + g, :], start=False, stop=True)
                        return ins
                    op("pe", mmy, reads=["vwT0", "vwT1", "cs128"], writes=["fY"])
                    op("act", lambda e, kt=kt: e.activation(
                        out=YTa[:, :, kt * 128:(kt + 1) * 128],
                        in_=fY[:].rearrange("p a (g k) -> p (a g) k", g=4), func=AF.Copy),
                       reads=["fY"], writes=[("YTa", kt)])
                    if kt < NT // 2:
                        def trr0(e):
                            for q in range(8):
                                ins = e.matmul(fT[:, q, :], lhsT=vw[:, 0, q * 128:(q + 1) * 128], rhs=antiid[:],
                                               start=True, stop=True)
                            return ins

                        def trr1(e):
                            for q in range(8):
                                ins = e.matmul(fT[:, q, :], lhsT=vw[:, 1, q * 128:(q + 1) * 128], rhs=antiid[:],
                                               start=True, stop=True)
                            return ins
                        op("pe", trr0, reads=["vw0", "antiid"], writes=["fT"])
                        op("act", lambda e: e.activation(out=vwTR[:, 0:8, :], in_=fT[:], func=AF.Copy),
                           reads=["fT"], writes=["vwTR0"])
                        op("pe", trr1, reads=["vw1", "antiid"], writes=["fT"])
                        op("dve", lambda e: e.tensor_copy(out=vwTR[:, 8:16, :], in_=fT[:]),
                           reads=["fT"], writes=["vwTR1"])

                        def mmy2(e):
                            for g in range(8):
                                o_ = fY[:, g // 4, (g % 4) * 128:(g % 4 + 1) * 128]
                                e.matmul(o_, lhsT=cs128[:, 0:128], rhs=vwTR[:, g, :], start=True, stop=False)
                                ins = e.matmul(o_, lhsT=cs128[:, 256:384], rhs=vwTR[:, 8 + g, :], start=False, stop=True)
                            return ins
                        op("pe", mmy2, reads=["vwTR0", "vwTR1", "cs128"], writes=["fY"])
                        st_ = S - kt * 128 - 127
                        ncol = 127 if kt == 0 else 128
                        op("act", lambda e, st_=st_, ncol=ncol: e.activation(
                            out=YTa[:, :, st_:st_ + ncol],
                            in_=fY[:].rearrange("p a (g k) -> p (a g) k", g=4)[:, :, 0:ncol], func=AF.Copy),
                           reads=["fY"], writes=["YTa_m"])
                dma(YT[:, 0:S].rearrange("(g p) s -> p g s", p=128), YTa[:],
                    reads=[("YTa", kt) for kt in range(NT // 2 + 1)] + ["YTa_m"], writes=["YT"])
                S_.flush()
                if upto == 'pf':
                    raise _Stop(nc)

            with _Alloc(nc) as A8:
                wso = A8.sb("wso", [128, 16, D], BF16)
                wfo = A8.sb("wfo", [128, 8, D], BF16)
                wo = A8.sb("wo", [128, 8, D], BF16)
                ynb0 = A8.sb("ynb0", [128, 16, 512], BF16)
                ynb1 = A8.sb("ynb1", [128, 16, 512], BF16)
                ytb0 = A8.sb("ytb0", [128, 8, 512], BF16)
                ytb1 = A8.sb("ytb1", [128, 8, 512], BF16)
                gtb0 = A8.sb("gtb0", [128, 16, 512], BF16)
                gtb1 = A8.sb("gtb1", [128, 16, 512], BF16)
                mT = A8.sb("mT", [128, 8, 512], BF16)
                m1 = A8.sb("m1", [128, 512], F32)
                m2 = A8.sb("m2", [128, 512], F32)
                xa0 = A8.sb("xa0", [128, D], F32)
                xa1 = A8.sb("xa1", [128, D], F32)
                xo0 = A8.sb("xo0", [128, D], F32)
                xo1 = A8.sb("xo1", [128, D], F32)
                gA = A8.ps("gA", [128, 2, 512], F32)
                gF = A8.ps("gF", [128, 2, 512], F32)
                gO = A8.ps("gO", [128, 2, 2, 512], F32)
                ynb = [ynb0, ynb1]; ytb = [ytb0, ytb1]; gtb = [gtb0, gtb1]; xa = [xa0, xa1]; xo = [xo0, xo1]
                for m_ in range(8):
                    dma(wso[:, :, m_ * 128:(m_ + 1) * 128], w_so_l[:, :, m_ * 128:(m_ + 1) * 128], writes=[("wso", m_)], eng="pool")
                    dma(wfo[:, :, m_ * 128:(m_ + 1) * 128], w_fo_l[:, :, m_ * 128:(m_ + 1) * 128], writes=[("wfo", m_)], eng="pool")
                dma(wo[:], w_o_l, writes=["wo"], eng="pool")
                nt_ = 0
                for blk in range(NB):
                    b = blk % 2
                    bs = slice(blk * 512, (blk + 1) * 512)
                    dma(ynb[b][:], ynT[:, bs].rearrange("(k p) s -> p k s", p=128), writes=[("ynb", b)])
                    dma(ytb[b][:], YT[:, bs].rearrange("(k p) s -> p k s", p=128), writes=[("ytb", b)])
                    dma(gtb[b][:], gatesT[:, bs].rearrange("(k p) s -> p k s", p=128), writes=[("gtb", b)])
                    for m in range(8):
                        pb = m % 2

                        def mma(e, b=b, m=m, pb=pb):
                            for cc in range(16):
                                ins = e.matmul(gA[:, pb, :], lhsT=wso[:, cc, m * 128:(m + 1) * 128], rhs=ynb[b][:, cc, :],
                                               start=(cc == 0), stop=(cc == 15))
                            return ins

                        def mmf2(e, b=b, m=m, pb=pb):
                            for cc in range(8):
                                ins = e.matmul(gF[:, pb, :], lhsT=wfo[:, cc, m * 128:(m + 1) * 128], rhs=ytb[b][:, cc, :],
                                               start=(cc == 0), stop=(cc == 7))
                            return ins
                        op("pe", mma, reads=[("wso", m), ("ynb", b)], writes=[("gA", pb)])
                        op("pe", mmf2, reads=[("wfo", m), ("ytb", b)], writes=[("gF", pb)])
                        op("dve", lambda e, b=b, m=m, pb=pb: e.tensor_tensor(out=m1[:], in0=gtb[b][:, m, :], in1=gA[:, pb, :],
                                                                             op=ALU.mult),
                           reads=[("gA", pb), ("gtb", b)], writes=["m1"])
                        op("act", lambda e, m=m, pb=pb: e.activation(out=m2[:], in_=gF[:, pb, :], func=AF.Identity,
                                                                     bias=bfo[:, m:m + 1]),
                           reads=[("gF", pb)], writes=["m2"])
                        op("pool", lambda e, b=b, m=m: e.tensor_tensor(out=m2[:], in0=m2[:], in1=gtb[b][:, 8 + m, :], op=ALU.mult),
                           reads=["m2", ("gtb", b)], writes=["m2"])
                        op("dve", lambda e, m=m: e.tensor_tensor(out=mT[:, m, :], in0=m1[:], in1=m2[:], op=ALU.add),
                           reads=["m1", "m2"], writes=[("mT", m)])
                    for tt in range(4):
                        t = blk * 4 + tt
                        xbuf = nt_ % 2
                        nt_ += 1
                        dma(xa[xbuf][:], x_seq[t * 128:(t + 1) * 128, :], writes=[("xa", xbuf)])

                        def mmo(e, tt=tt, xbuf=xbuf):
                            for hf in range(2):
                                for cc in range(8):
                                    ins = e.matmul(gO[:, xbuf, hf, :], lhsT=mT[:, cc, tt * 128:(tt + 1) * 128],
                                                   rhs=wo[:, cc, hf * 512:(hf + 1) * 512], start=(cc == 0), stop=(cc == 7))
                            return ins
                        op("pe", mmo, reads=["wo"] + [("mT", m) for m in range(8)], writes=[("gO", xbuf)])
                        op("dve", lambda e, xbuf=xbuf: e.tensor_tensor(out=xo[xbuf][:], in0=xa[xbuf][:],
                                                                       in1=gO[:, xbuf].rearrange("p a b -> p (a b)"), op=ALU.add),
                           reads=[("xa", xbuf), ("gO", xbuf)], writes=[("xo", xbuf)])
                        dma(x2d[t * 128:(t + 1) * 128, :], xo[xbuf][:], reads=[("xo", xbuf)], writes=[("x2d", t)])
                S_.flush()
                if upto == 'p3a':
                    raise _Stop(nc)

            with _Alloc(nc) as A9:
                wgu = A9.sb("wgu", [128, 8, 2 * DFF], BF16)
                wd = A9.sb("wd", [128, 22, D], BF16)
                gfin = A9.sb("gfin", [128, D], F32)
                xq = [A9.sb("xq%d" % k_, [128, D], F32) for k_ in range(4)]
                xm = [A9.sb("xm%d" % k_, [128, D], BF16) for k_ in range(2)]
                fs = [A9.sb("fs%d" % k_, [128, 1], F32) for k_ in range(2)]
                fd = [A9.sb("fd%d" % k_, [128, 1], F32) for k_ in range(2)]
                fr = [A9.sb("fr%d" % k_, [128, 1], F32) for k_ in range(2)]
                h2T = A9.sb("h2T", [128, 8, 512], BF16)
                aT = A9.sb("aT", [128, 22, 512], BF16)
                sgs = [A9.sb("sg%d" % k_, [128, 512], F32) for k_ in range(2)]
                s3 = A9.sb("s3", [128, 1], F32)
                d3 = A9.sb("d3", [128, 1], F32)
                r3 = A9.sb("r3", [128, 1], F32)
                ob = [A9.sb("ob%d" % k_, [128, D], F32) for k_ in range(2)]
                hP = A9.ps("hP", [128, 2, 8, 128], BF16)
                hG = A9.ps("hG", [128, 4, 512], F32)
                hD = A9.ps("hD", [128, 1, 2, 512], F32)
                for j_ in range(11):
                    c0_ = j_ * 256
                    dma(wgu[:, :, c0_:c0_ + 256], w_gu_l[:, :, c0_:c0_ + 256], writes=[("wgu", j_, 0)], eng="pool")
                    dma(wgu[:, :, DFF + c0_:DFF + c0_ + 256], w_gu_l[:, :, DFF + c0_:DFF + c0_ + 256], writes=[("wgu", j_, 1)],
                        eng="pool")
                dma(wd[:], w_d_l, writes=["wd"], eng="pool")
                dma(gfin[:], gfin_bc, writes=["gfin"])
                for blk in range(S // 512):
                    for tt in range(4):
                        t = blk * 4 + tt
                        b2 = tt % 2
                        dma(xq[tt][:], x2d[t * 128:(t + 1) * 128, :], writes=[("xq", tt)])
                        rms_to_T("p3", b2, xq[tt][:], xm[b2], fs[b2], fd[b2], fr[b2], hP[:, b2], gffn,
                                 h2T[:, :, tt * 128:(tt + 1) * 128], [("xq", tt)], ("h2T", tt))
                    for i in range(22):
                        q = i % 2

                        def mmg(e, i=i, q=q):
                            for cc in range(8):
                                e.matmul(hG[:, 2 * q, :], lhsT=wgu[:, cc, i * 128:(i + 1) * 128], rhs=h2T[:, cc, :],
                                         start=(cc == 0), stop=(cc == 7))
                            for cc in range(8):
                                ins = e.matmul(hG[:, 2 * q + 1, :], lhsT=wgu[:, cc, DFF + i * 128:DFF + (i + 1) * 128],
                                               rhs=h2T[:, cc, :], start=(cc == 0), stop=(cc == 7))
                            return ins
                        op("pe", mmg, reads=[("wgu", i // 2, 0), ("wgu", i // 2, 1)] + [("h2T", k_) for k_ in range(4)],
                           writes=[("hGg", q), ("hGu", q)])
                        op("act", lambda e, q=q: e.activation(out=sgs[q][:], in_=hG[:, 2 * q, :], func=AF.Silu),
                           reads=[("hGg", q)], writes=[("sg", q)])
                        op("dve", lambda e, i=i, q=q: e.tensor_tensor(out=aT[:, i, :], in0=sgs[q][:], in1=hG[:, 2 * q + 1, :],
                                                                      op=ALU.mult),
                           reads=[("sg", q), ("hGu", q)], writes=[("aT", i)])
                    for tt in range(4):
                        t = blk * 4 + tt
                        b2 = tt % 2

                        def mmd2(e, tt=tt):
                            for hf in range(2):
                                for i in range(22):
                                    ins = e.matmul(hD[:, 0, hf, :], lhsT=aT[:, i, tt * 128:(tt + 1) * 128],
                                                   rhs=wd[:, i, hf * 512:(hf + 1) * 512], start=(i == 0), stop=(i == 21))
                            return ins
                        op("pe", mmd2, reads=["wd"] + [("aT", i) for i in range(22)], writes=["hD"])
                        op("dve", lambda e, tt=tt: e.tensor_tensor(out=xq[tt][:], in0=xq[tt][:],
                                                                   in1=hD[:, 0].rearrange("p a b -> p (a b)"), op=ALU.add),
                           reads=["hD"], writes=[("xq", tt), "hD"])
                        op("act", lambda e, tt=tt, b2=b2: e.activation(out=ob[b2][:], in_=xq[tt][:], func=AF.Square,
                                                                        accum_out=s3[:]),
                           reads=[("xq", tt)], writes=[("ob", b2), "s3"])
                        op("act", lambda e: e.activation(out=d3[:], in_=s3[:], func=AF.Sqrt, scale=1.0 / D, bias=EPS),
                           reads=["s3"], writes=["d3"])
                        op("dve", lambda e: e.reciprocal(out=r3[:], in_=d3[:]), reads=["d3"], writes=["r3"])
                        op("act", lambda e, tt=tt, b2=b2: e.activation(out=ob[b2][:], in_=xq[tt][:], func=AF.Copy, scale=r3[:]),
                           reads=[("xq", tt), "r3"], writes=[("ob", b2)])
                        op("dve", lambda e, b2=b2: e.tensor_tensor(out=ob[b2][:], in0=ob[b2][:], in1=gfin[:], op=ALU.mult),
                           reads=["gfin"], writes=[("ob", b2)])
                        dma(y_seq[t * 128:(t + 1) * 128, :], ob[b2][:], reads=[("ob", b2)], writes=[("y", t)])
                S_.flush()
                if upto == 'p3b':
                    raise _Stop(nc)
            s0 += S
    return nc


def _tile_w(w):
    k, n = w.shape
    return np.ascontiguousarray(w.reshape(k // 128, 128, n).transpose(1, 0, 2))


def _col(v):
    return np.ascontiguousarray(v.reshape(-1, 128).T)


def _bcast(v):
    return np.ascontiguousarray(np.broadcast_to(v.reshape(1, -1), (128, v.size)))


_CONST_CACHE = {}


def _consts(uniqS):
    key = tuple(uniqS)
    if key in _CONST_CACHE:
        return _CONST_CACHE[key]
    bf = ml_dtypes.bfloat16
    j = np.arange(128)[:, None]
    l = np.arange(128)[None, :]
    Lle = (j <= l).astype(np.float32)
    Lgt = (j > l).astype(np.float32)
    Lge = (j >= l).astype(np.float32)
    Llt = (j < l).astype(np.float32)
    ones = np.ones((128, 128), np.float32)
    c = {}
    c["ident_b"] = np.eye(128, dtype=np.float32).astype(bf)
    c["ident_f"] = np.eye(128, dtype=np.float32)
    c["antiid_b"] = np.ascontiguousarray(np.eye(128, dtype=np.float32)[:, ::-1]).astype(bf)
    c["tri_f"] = np.ascontiguousarray(np.stack([Lle, Lgt, Lge, Llt, ones], axis=1))
    c["tri_b"] = np.ascontiguousarray(np.stack([Lle, Lgt, Lge, Llt], axis=1)).astype(bf)
    negf = NEG * (l < j).astype(np.float32)
    negb = NEG * (l > j).astype(np.float32)
    c["neg_b"] = np.ascontiguousarray(np.stack([np.tile(negf, (1, 4)), np.tile(negb, (1, 4))], axis=1)).astype(bf)
    for S in uniqS:
        NT = S // 128
        sc = 1.0 / math.sqrt(S * 128.0)
        ang = 2.0 * np.pi * (np.arange(128)[:, None] * np.arange(128)[None, :] % 128) / 128.0
        c["cs128_%d" % S] = np.concatenate([np.cos(ang) * sc, -np.sin(ang) * sc, np.sin(ang) * sc], axis=1).astype(np.float32).astype(bf)
        n = np.arange(S, dtype=np.int64)
        prod = (n[:, None] * n[None, :]) % S
        a = 2.0 * np.pi * prod.astype(np.float64) / S
        cs = np.stack([np.cos(a), np.sin(a)], axis=0).astype(np.float32)
        tab = cs.reshape(2, NT, 128, NT, 128).transpose(3, 2, 1, 0, 4)
        c["tab_%d" % S] = np.ascontiguousarray(tab).astype(bf)
    _CONST_CACHE[key] = c
    return c


def _shared_inputs(norm_mix, w_in, conv_w, conv_b, dt_bias_f, dt_bias_b, a_log_f, a_log_b, d_skip, ssd_norm,
                   w_ssd_out, w_fourier_out, b_fourier_out, w_out, norm_ffn, w_gate_up, w_down, norm_final, uniqS):
    f = lambda a: np.asarray(a, dtype=np.float32)
    m = {}
    m["w_in_l"] = _tile_w(f(w_in)[0])
    m["w_so_l"] = _tile_w(f(w_ssd_out)[0])
    m["w_fo_l"] = _tile_w(f(w_fourier_out)[0])
    m["w_o_l"] = _tile_w(f(w_out)[0])
    m["w_gu_l"] = _tile_w(f(w_gate_up)[0])
    m["w_d_l"] = _tile_w(f(w_down)[0])
    m["convw_l"] = np.ascontiguousarray(f(conv_w)[0].T.reshape(32, 128, 7).transpose(1, 0, 2))
    m["convb_l"] = _col(f(conv_b)[0])
    dtb = np.concatenate([f(dt_bias_f)[0], f(dt_bias_b)[0]])
    m["dtb_col"] = np.ascontiguousarray(np.concatenate([dtb, dtb]).reshape(128, 1))
    al = np.concatenate([f(a_log_f)[0], f(a_log_b)[0]])
    m["alog_col"] = np.ascontiguousarray(np.concatenate([al, al]).reshape(128, 1))
    m["dskip_bc"] = _bcast(f(d_skip)[0])
    m["ssdn_bc"] = _bcast(f(ssd_norm)[0])
    m["gmix_col"] = _col(f(norm_mix)[0])
    m["gffn_col"] = _col(f(norm_ffn)[0])
    m["gfin_bc"] = _bcast(f(norm_final))
    m["bfo_col"] = _col(f(b_fourier_out)[0])
    m.update(_consts(uniqS))
    return m


def kernel(x_prompt, x_sample, norm_mix, w_in, conv_w, conv_b, dt_bias_f, dt_bias_b, a_log_f, a_log_b,
           d_skip, ssd_norm, w_ssd_out, w_fourier_out, b_fourier_out, w_out, norm_ffn, w_gate_up, w_down,
           norm_final):
    xp = np.asarray(x_prompt, dtype=np.float32)
    xs = np.asarray(x_sample, dtype=np.float32)
    n = 8
    Sp, Ss = xp.shape[1], xs.shape[1]
    seq_lens = [Sp, Ss, Ss]
    shared = _shared_inputs(norm_mix, w_in, conv_w, conv_b, dt_bias_f, dt_bias_b, a_log_f, a_log_b, d_skip,
                            ssd_norm, w_ssd_out, w_fourier_out, b_fourier_out, w_out, norm_ffn, w_gate_up,
                            w_down, norm_final, sorted(set(seq_lens)))
    nc = build(seq_lens)
    in_maps = []
    for i in range(n):
        m = dict(shared)
        m["x_all"] = np.ascontiguousarray(np.concatenate([xp[i], xs[2 * i], xs[2 * i + 1]], axis=0))
        in_maps.append(m)
    res = run_bass_kernel_spmd(nc, in_maps, core_ids=list(range(n)))
    yp = np.empty_like(xp)
    ys = np.empty_like(xs)
    for i in range(n):
        y = np.asarray(res.results[i]["y_all"], dtype=np.float32)
        yp[i] = y[0:Sp]
        ys[2 * i] = y[Sp:Sp + Ss]
        ys[2 * i + 1] = y[Sp + Ss:Sp + 2 * Ss]
    return (yp, ys)
```
